# Optimizing a Trainium2 kernel written in Bass

```python
import math
import jax, jax.numpy as jnp
from jax import lax
import numpy as np

D_MODEL = 1024
BATCH = 4
SEQ = 4096
DEPTH = 2
DEC_BATCH = 16
DEC_SEQ = 16
PAST_LEN = 1024

CHUNK = 64
EPS = 1e-6
NEG_INF = -1e30
N_HEADS = 8
N_KV_HEADS = 2
HEAD_DIM = 64
Q_PER_KV = N_HEADS // N_KV_HEADS
ATT_WIDTH = N_HEADS * HEAD_DIM
KV_WIDTH = N_KV_HEADS * HEAD_DIM
WINDOW = 128
WIN_CHUNKS = WINDOW // CHUNK
NUM_BUCKETS = 32
MAX_DISTANCE = 128
SSD_INNER = D_MODEL
SSD_HEAD_DIM = 64
SSD_HEADS = SSD_INNER // SSD_HEAD_DIM
SSD_GROUPS = 2
SSD_HPG = SSD_HEADS // SSD_GROUPS
SSD_STATE = 128
CONV_W = 4
CONV_DIM = SSD_INNER + 2 * SSD_GROUPS * SSD_STATE
SSD_BLOCK = CHUNK
DT_MIN = 1e-3
DT_MAX = 1e-1
D_FF = -(-8 * D_MODEL // (3 * 256)) * 256
IN_SPLITS = (ATT_WIDTH, KV_WIDTH, KV_WIDTH, SSD_INNER, CONV_DIM, SSD_HEADS, D_MODEL, D_MODEL)
IN_WIDTH = sum(IN_SPLITS)

kernel_name = 'hybrid_swa_sink_ssd_stream_encoder_step'


def rmsnorm(x, g):
    xf = x.astype(jnp.float32)
    y = xf * lax.rsqrt(jnp.mean(xf * xf, axis=-1, keepdims=True) + EPS)
    return (y * g.astype(jnp.float32)).astype(x.dtype)


def t5_bucket(rel):
    nb = NUM_BUCKETS // 2
    max_exact = nb // 2
    ret = jnp.where(rel > 0, nb, 0)
    n = jnp.abs(rel)
    nf = jnp.maximum(n, 1).astype(jnp.float32)
    large = max_exact + (jnp.log(nf / max_exact) / math.log(MAX_DISTANCE / max_exact)
                         * (nb - max_exact)).astype(jnp.int32)
    large = jnp.minimum(large, nb - 1)
    return ret + jnp.where(n < max_exact, n, large)


def rel_bias(rel, table):
    b = table[t5_bucket(rel)].astype(jnp.float32)
    return jnp.moveaxis(b, -1, 0).reshape(N_KV_HEADS, Q_PER_KV, rel.shape[0], rel.shape[1])


def sink_attention(q, k, v, bias, sinks, kmask):
    logits = jnp.einsum('bxqgrd,bxkgd->bxgrqk', q.astype(jnp.float32), k.astype(jnp.float32))
    logits = logits * (HEAD_DIM ** -0.5) + bias
    if kmask is not None:
        logits = jnp.where(kmask[:, :, None, None, None, :], logits, NEG_INF)
    s = sinks.astype(jnp.float32).reshape(1, 1, N_KV_HEADS, Q_PER_KV, 1, 1)
    m = jnp.maximum(jnp.max(logits, axis=-1, keepdims=True), s)
    p = jnp.exp(logits - m)
    denom = jnp.sum(p, axis=-1, keepdims=True) + jnp.exp(s - m)
    return jnp.einsum('bxgrqk,bxkgd->bxqgrd', p / denom, v.astype(jnp.float32))


def attn_prompt(q, k, v, table, sinks, kv_len):
    b, s = q.shape[0], q.shape[1]
    nc = s // CHUNK
    pad = WIN_CHUNKS * CHUNK
    lk = pad + CHUNK
    qb = q.reshape(b, nc, CHUNK, N_KV_HEADS, Q_PER_KV, HEAD_DIM)

    def band(t):
        tp = jnp.pad(t, ((0, 0), (pad, 0), (0, 0), (0, 0)))
        tp = tp.reshape(b, nc + WIN_CHUNKS, CHUNK, N_KV_HEADS, HEAD_DIM)
        return jnp.concatenate([tp[:, i:i + nc] for i in range(WIN_CHUNKS + 1)], axis=2)

    qi = jnp.arange(CHUNK, dtype=jnp.int32)
    kj = jnp.arange(lk, dtype=jnp.int32)
    bias = rel_bias(kj[None, :] - pad - qi[:, None], table)
    kpos = (jnp.arange(nc, dtype=jnp.int32)[:, None] - WIN_CHUNKS) * CHUNK + kj[None, :]
    o = sink_attention(qb, band(k), band(v), bias, sinks, (kpos >= 0)[None])
    return o.reshape(b, s, ATT_WIDTH), k[:, s - kv_len:], v[:, s - kv_len:]


def attn_sample(q, k, v, k_cache, v_cache, table, sinks):
    b, t = q.shape[0], q.shape[1]
    kv_len = k_cache.shape[1]
    kk = jnp.concatenate([k_cache.astype(k.dtype), k], axis=1)
    vv = jnp.concatenate([v_cache.astype(v.dtype), v], axis=1)
    qpos = PAST_LEN + jnp.arange(t, dtype=jnp.int32)
    kpos = PAST_LEN - kv_len + jnp.arange(kv_len + t, dtype=jnp.int32)
    bias = rel_bias(kpos[None, :] - qpos[:, None], table)
    o = sink_attention(q.reshape(b, 1, t, N_KV_HEADS, Q_PER_KV, HEAD_DIM),
                       kk[:, None], vv[:, None], bias, sinks, None)
    return o.reshape(b, t, ATT_WIDTH), kk[:, t:], vv[:, t:]


def causal_conv(xbc, hist, w, bias):
    xp = jnp.concatenate([hist.astype(xbc.dtype), xbc], axis=1)
    out = lax.conv_general_dilated(xp, w[:, None, :].astype(xp.dtype), window_strides=(1,),
                                   padding='VALID', dimension_numbers=('NWC', 'WIO', 'NWC'),
                                   feature_group_count=CONV_DIM)
    return jax.nn.silu(out + bias.astype(xp.dtype)), xp[:, -(CONV_W - 1):]


def ssd_scan(x, dt, a, bm, cm, h0, block):
    b, s = x.shape[0], x.shape[1]
    nc = s // block
    f32 = jnp.float32
    xr = x.astype(f32).reshape(b, nc, block, SSD_GROUPS, SSD_HPG, SSD_HEAD_DIM)
    dtr = dt.reshape(b, nc, block, SSD_GROUPS, SSD_HPG)
    br = bm.astype(f32).reshape(b, nc, block, SSD_GROUPS, SSD_STATE)
    cr = cm.astype(f32).reshape(b, nc, block, SSD_GROUPS, SSD_STATE)
    acum = jnp.cumsum(dtr * a.reshape(SSD_GROUPS, SSD_HPG), axis=2)
    ac = jnp.moveaxis(acum, 2, -1)
    causal = jnp.tril(jnp.ones((block, block), dtype=bool))
    lmat = jnp.exp(jnp.where(causal, ac[..., :, None] - ac[..., None, :], -jnp.inf))
    cb = jnp.einsum('bclgn,bcsgn->bcgls', cr, br)
    y_diag = jnp.einsum('bcgls,bcghls,bcsgh,bcsghp->bclghp', cb, lmat, dtr, xr)
    decay_end = jnp.exp(ac[..., -1:] - ac)
    st = jnp.einsum('bclgn,bcghl,bclgh,bclghp->bcghpn', br, decay_end, dtr, xr)
    blk_decay = jnp.exp(ac[..., -1])

    def step(h, inp):
        s_c, d_c = inp
        return d_c[..., None, None] * h + s_c, h

    h_init = h0.astype(f32).reshape(b, SSD_GROUPS, SSD_HPG, SSD_HEAD_DIM, SSD_STATE)
    h_fin, h_prev = lax.scan(step, h_init, (jnp.moveaxis(st, 1, 0), jnp.moveaxis(blk_decay, 1, 0)))
    h_prev = jnp.moveaxis(h_prev, 0, 1)
    y_off = jnp.einsum('bclgn,bcghpn,bcghl->bclghp', cr, h_prev, jnp.exp(ac))
    y = (y_diag + y_off).reshape(b, s, SSD_HEADS, SSD_HEAD_DIM)
    return y, h_fin.reshape(b, SSD_HEADS, SSD_HEAD_DIM, SSD_STATE)


def trunk_layer(x, attn_fn, conv_hist, ssm0, ssd_block, g_mix, w_in, conv_w, conv_b, dt_bias, a_log,
                d_skip, g_ssd, w_att_out, w_ssd_out, w_out, g_ffn, w_gate, w_up, w_down):
    b, t = x.shape[0], x.shape[1]
    f32 = jnp.float32
    hn = rmsnorm(x, g_mix)
    u = hn @ w_in
    q, k, v, z, xbc, dt_raw, gate_a, gate_s = jnp.split(u, np.cumsum(IN_SPLITS)[:-1].tolist(), axis=-1)
    o_att, k_state, v_state = attn_fn(q, k.reshape(b, t, N_KV_HEADS, HEAD_DIM),
                                      v.reshape(b, t, N_KV_HEADS, HEAD_DIM))
    xbc, conv_state = causal_conv(xbc, conv_hist, conv_w, conv_b)
    xs, bm, cm = jnp.split(xbc, [SSD_INNER, SSD_INNER + SSD_GROUPS * SSD_STATE], axis=-1)
    dt = jax.nn.softplus(dt_raw.astype(f32) + dt_bias.astype(f32))
    a = -jnp.exp(a_log.astype(f32))
    xh = xs.reshape(b, t, SSD_HEADS, SSD_HEAD_DIM)
    y, ssm_state = ssd_scan(xh, dt, a, bm.reshape(b, t, SSD_GROUPS, SSD_STATE),
                            cm.reshape(b, t, SSD_GROUPS, SSD_STATE), ssm0, ssd_block)
    y = y + xh.astype(f32) * d_skip.astype(f32)[:, None]
    y = y.reshape(b, t, SSD_INNER) * jax.nn.silu(z.astype(f32))
    y_ssd = rmsnorm(y, g_ssd).astype(x.dtype)
    merged = (jax.nn.sigmoid(gate_a) * (o_att.astype(x.dtype) @ w_att_out)
              + jax.nn.sigmoid(gate_s) * (y_ssd @ w_ssd_out))
    h = x + merged @ w_out
    hf = rmsnorm(h, g_ffn)
    out = h + (jax.nn.silu(hf @ w_gate) * (hf @ w_up)) @ w_down
    return out, (k_state, v_state, conv_state, ssm_state.astype(x.dtype))


def setup_inputs(seed: int = 0) -> dict:
    key = jax.random.key(seed)
    k = jax.random.split(key, 24)
    f32 = jnp.float32
    kv_len = min(WINDOW, PAST_LEN)

    def nrm(kk, shape, scale):
        return jax.random.normal(kk, shape, f32) * scale

    dt0 = jnp.exp(jax.random.uniform(k[10], (DEPTH, SSD_HEADS), f32, math.log(DT_MIN), math.log(DT_MAX)))
    return {
        'x_prompt': nrm(k[0], (BATCH, SEQ, D_MODEL), 1.0),
        'x_sample': nrm(k[1], (DEC_BATCH, DEC_SEQ, D_MODEL), 1.0),
        'cache_k': nrm(k[2], (DEPTH, DEC_BATCH, kv_len, N_KV_HEADS, HEAD_DIM), 1.0),
        'cache_v': nrm(k[3], (DEPTH, DEC_BATCH, kv_len, N_KV_HEADS, HEAD_DIM), 1.0),
        'state_conv': nrm(k[4], (DEPTH, DEC_BATCH, CONV_W - 1, CONV_DIM), 1.0),
        'state_ssm': nrm(k[5], (DEPTH, DEC_BATCH, SSD_HEADS, SSD_HEAD_DIM, SSD_STATE), 0.1),
        'rel_table': nrm(k[6], (NUM_BUCKETS, N_HEADS), 0.5),
        'g_mix': 1.0 + nrm(k[7], (DEPTH, D_MODEL), 0.02),
        'w_in': nrm(k[8], (DEPTH, D_MODEL, IN_WIDTH), D_MODEL ** -0.5),
        'conv_w': nrm(k[9], (DEPTH, CONV_W, CONV_DIM), CONV_W ** -0.5),
        'conv_b': nrm(k[11], (DEPTH, CONV_DIM), 0.02),
        'dt_bias': dt0 + jnp.log(-jnp.expm1(-dt0)),
        'a_log': jnp.log(jax.random.uniform(k[12], (DEPTH, SSD_HEADS), f32, 1.0, 16.0)),
        'd_skip': 1.0 + nrm(k[13], (DEPTH, SSD_HEADS), 0.1),
        'g_ssd': 1.0 + nrm(k[14], (DEPTH, SSD_INNER), 0.02),
        'sinks': nrm(k[15], (DEPTH, N_HEADS), 0.5),
        'w_att_out': nrm(k[16], (DEPTH, ATT_WIDTH, D_MODEL), ATT_WIDTH ** -0.5),
        'w_ssd_out': nrm(k[17], (DEPTH, SSD_INNER, D_MODEL), SSD_INNER ** -0.5),
        'w_out': nrm(k[18], (DEPTH, D_MODEL, D_MODEL), D_MODEL ** -0.5),
        'g_ffn': 1.0 + nrm(k[19], (DEPTH, D_MODEL), 0.02),
        'w_gate': nrm(k[20], (DEPTH, D_MODEL, D_FF), D_MODEL ** -0.5),
        'w_up': nrm(k[21], (DEPTH, D_MODEL, D_FF), D_MODEL ** -0.5),
        'w_down': nrm(k[22], (DEPTH, D_FF, D_MODEL), D_FF ** -0.5),
        'g_final': 1.0 + nrm(k[23], (D_MODEL,), 0.02),
    }


def reference(x_prompt, x_sample, cache_k, cache_v, state_conv, state_ssm, rel_table, g_mix, w_in,
              conv_w, conv_b, dt_bias, a_log, d_skip, g_ssd, sinks, w_att_out, w_ssd_out, w_out,
              g_ffn, w_gate, w_up, w_down, g_final):
    kv_len = cache_k.shape[2]
    xp, xs = x_prompt, x_sample
    bp = xp.shape[0]
    st_p, st_s = [], []
    for l in range(DEPTH):
        lw = (g_mix[l], w_in[l], conv_w[l], conv_b[l], dt_bias[l], a_log[l], d_skip[l], g_ssd[l],
              w_att_out[l], w_ssd_out[l], w_out[l], g_ffn[l], w_gate[l], w_up[l], w_down[l])
        sl = sinks[l]
        xp, sp = trunk_layer(
            xp, lambda q, k, v, sl=sl: attn_prompt(q, k, v, rel_table, sl, kv_len),
            jnp.zeros((bp, CONV_W - 1, CONV_DIM), xp.dtype),
            jnp.zeros((bp, SSD_HEADS, SSD_HEAD_DIM, SSD_STATE), jnp.float32),
            SSD_BLOCK, *lw)
        xs, ss = trunk_layer(
            xs, lambda q, k, v, sl=sl, kc=cache_k[l], vc=cache_v[l]: attn_sample(q, k, v, kc, vc, rel_table, sl),
            state_conv[l], state_ssm[l], xs.shape[1], *lw)
        st_p.append(sp)
        st_s.append(ss)
    y_prompt = rmsnorm(xp, g_final)
    y_sample = rmsnorm(xs, g_final)
    k_prompt = jnp.stack([s[0] for s in st_p])
    v_prompt = jnp.stack([s[1] for s in st_p])
    conv_prompt = jnp.stack([s[2] for s in st_p])
    ssm_prompt = jnp.stack([s[3] for s in st_p])
    k_sample = jnp.stack([s[0] for s in st_s])
    v_sample = jnp.stack([s[1] for s in st_s])
    conv_sample = jnp.stack([s[2] for s in st_s])
    ssm_sample = jnp.stack([s[3] for s in st_s])
    return (y_prompt, y_sample, k_prompt, v_prompt, conv_prompt, ssm_prompt,
            k_sample, v_sample, conv_sample, ssm_sample)
```

```python
import os
import numpy as np
from contextlib import ExitStack
import concourse.bass as bass
import concourse.mybir as mybir
from concourse.bass_utils import run_bass_kernel_spmd

F32 = mybir.dt.float32
BF16 = mybir.dt.bfloat16
AF = mybir.ActivationFunctionType
ALU = mybir.AluOpType

D = 1024
KC = 8
T = 512
NSUB = 4
DEPTH = 2
INW = 5392
DFF = 2816
EPS = 1e-6
NCORES = 8
SEQ_FULL = 4096
NSLOT = 3
SLOTW = 4096


def OP(name, *args, **kwargs):
    def f(e):
        return getattr(e, name)(*args, **kwargs)
    return f


class Slot:
    def __init__(self, sem):
        self.sem = sem
        self.val = 0


class Sched:
    ENG = ("pe", "dve", "act", "pool", "sp")

    def __init__(self, nc, sems):
        self.nc = nc
        self.sem = dict(zip(self.ENG, sems))
        self.cnt = {e: 0 for e in self.ENG}
        self.ops = {e: [] for e in self.ENG}
        self.seen = {e: {} for e in self.ENG}
        self.last_w = {}
        self.readers = {}
        self.semobj = {}

    def _tid(self, sem):
        i = id(sem)
        self.semobj[i] = sem
        return i

    def _deps(self, eng, r, w):
        need = {}

        def add(tok):
            if tok is None:
                return
            sid, val, src = tok
            if src == "pe" and eng == "pe":
                return
            if need.get(sid, 0) < val:
                need[sid] = val

        for k in r:
            add(self.last_w.get(k))
        for k in w:
            add(self.last_w.get(k))
            for tok in self.readers.get(k, ()):
                add(tok)
        waits = []
        seen = self.seen[eng]
        for sid, val in need.items():
            if seen.get(sid, 0) < val:
                seen[sid] = val
                waits.append((self.semobj[sid], val))
        return waits

    def _commit(self, tok, r, w):
        for k in r:
            lst = self.readers.setdefault(k, [])
            lst[:] = [t for t in lst if t[0] != tok[0]]
            lst.append(tok)
        for k in w:
            self.last_w[k] = tok
            self.readers[k] = []

    def op(self, eng, fn, r=(), w=()):
        self.nrec = getattr(self, "nrec", 0) + 1
        if self.nrec > int(os.environ.get("KCUT", "100000000")):
            return
        waits = self._deps(eng, r, w)
        self.cnt[eng] += 1
        sem = self.sem[eng]
        tok = (self._tid(sem), self.cnt[eng], eng)
        self._commit(tok, r, w)

        def emit(e, waits=waits, fn=fn, sem=sem):
            for s, v in waits:
                e.wait_ge(s, v)
            fn(e).then_inc(sem, 1)

        self.ops[eng].append(emit)

    def dma(self, eng, slot, out, in_, r=(), w=(), serial=False, batch=False, **kw):
        self.nrec = getattr(self, "nrec", 0) + 1
        if self.nrec > int(os.environ.get("KCUT", "100000000")):
            return
        waits = self._deps(eng, r, w)
        if serial and slot.val:
            sid = self._tid(slot.sem)
            if self.seen[eng].get(sid, 0) < slot.val:
                self.seen[eng][sid] = slot.val
                waits.append((slot.sem, slot.val))
        slot.val += 16
        tok = (self._tid(slot.sem), slot.val, "dma")
        self._commit(tok, r, w)
        if batch:
            slot.bkeys = getattr(slot, "bkeys", set()) | set(w)

        def emit(e, waits=waits, slot=slot, out=out, in_=in_, kw=kw):
            for s, v in waits:
                e.wait_ge(s, v)
            e.dma_start(out=out, in_=in_, **kw).then_inc(slot.sem, 16)

        self.ops[eng].append(emit)

    def close_batch(self, slot):
        tok = (self._tid(slot.sem), slot.val, "dma")
        for k in getattr(slot, "bkeys", ()):
            self.last_w[k] = tok
        slot.bkeys = set()

    def final_wait(self, eng, slots):
        def emit(e, slots=slots):
            for s in slots:
                if s.val:
                    e.wait_ge(s.sem, s.val)

        self.ops[eng].append(emit)

    def emit_all(self):
        with self.nc.Block() as block:

            @block.tensor
            def _(e):
                for f in self.ops["pe"]:
                    f(e)

            @block.vector
            def _(e):
                for f in self.ops["dve"]:
                    f(e)

            @block.scalar
            def _(e):
                for f in self.ops["act"]:
                    f(e)

            @block.gpsimd
            def _(e):
                for f in self.ops["pool"]:
                    f(e)

            @block.sync
            def _(e):
                for f in self.ops["sp"]:
                    f(e)


def layer_units():
    u = [("q", "w_in", 1024, 0, 512), ("kv", "w_in", 1024, 512, 512),
         ("z0", "w_in", 1024, 768, 512), ("z1", "w_in", 1024, 1280, 512),
         ("xs0", "w_in", 1024, 1792, 512), ("xs1", "w_in", 1024, 2304, 512),
         ("bc", "w_in", 1024, 2816, 512), ("dt", "w_in", 1024, 3328, 16),
         ("ga0", "w_in", 1024, 3344, 512), ("ga1", "w_in", 1024, 3856, 512),
         ("gs0", "w_in", 1024, 4368, 512), ("gs1", "w_in", 1024, 4880, 512),
         ("ao0", "w_att_out", 512, 0, 512), ("so0", "w_ssd_out", 1024, 0, 512),
         ("ao1", "w_att_out", 512, 512, 512), ("so1", "w_ssd_out", 1024, 512, 512),
         ("wo0", "w_out", 1024, 0, 512), ("wo1", "w_out", 1024, 512, 512)]
    for j in range(6):
        n = 512 if j < 5 else 256
        u.append((f"g{j}", "w_gate", 1024, j * 512, n))
        u.append((f"u{j}", "w_up", 1024, j * 512, n))
    for oc in range(8):
        u.append((f"d{oc}", "w_down", 2816, oc * 128, 128))
    return u


def static_consts():
    c = {}
    i = np.arange(128)
    c["c_ident"] = np.eye(128, dtype=np.float32)
    c["c_J"] = np.eye(128, dtype=np.float32)[::-1].copy()
    c["c_Uinc"] = (i[:, None] <= i[None, :]).astype(np.float32)
    c["c_Ustr"] = (i[:, None] > i[None, :]).astype(np.float32)
    c["c_maskU"] = (i[:, None] <= i[None, :]).astype(np.float32)
    rel = 127 - np.arange(383)
    nb, me = 16, 8
    n = np.abs(rel)
    nf = np.maximum(n, 1).astype(np.float32)
    large = me + (np.log(nf / me) / np.log(128 / me) * (nb - me)).astype(np.int32)
    large = np.minimum(large, nb - 1)
    bucket = np.where(rel > 0, nb, 0) + np.where(n < me, n, large)
    c["c_bucket"] = bucket.astype(np.int64)
    oh = np.zeros((32, 384), dtype=np.float32)
    oh[bucket, np.arange(383)] = 1.0
    c["c_OH"] = oh
    return c


def build(seq, nsamp, bucket):
    NT = seq // T
    nc = bass.Bass("TRN2", target_bir_lowering=False)
    dt_ = lambda name, shape, dtype=F32, kind="ExternalInput": nc.dram_tensor(name, list(shape), dtype, kind=kind).ap()
    x_prompt = dt_("x_prompt", [seq, D])
    rel_table = dt_("rel_table", [32, 8])
    g_mix = dt_("g_mix", [DEPTH, D])
    w_in = dt_("w_in", [DEPTH, D, INW])
    conv_w = dt_("conv_w", [DEPTH, 4, 1536])
    conv_b = dt_("conv_b", [DEPTH, 1536])
    dt_bias = dt_("dt_bias", [DEPTH, 16])
    a_log = dt_("a_log", [DEPTH, 16])
    d_skip = dt_("d_skip", [DEPTH, 16])
    g_ssd = dt_("g_ssd", [DEPTH, D])
    sinks = dt_("sinks", [DEPTH, 8])
    W = {"w_in": w_in,
         "w_att_out": dt_("w_att_out", [DEPTH, 512, D]),
         "w_ssd_out": dt_("w_ssd_out", [DEPTH, D, D]),
         "w_out": dt_("w_out", [DEPTH, D, D]),
         "w_gate": dt_("w_gate", [DEPTH, D, DFF]),
         "w_up": dt_("w_up", [DEPTH, D, DFF]),
         "w_down": dt_("w_down", [DEPTH, DFF, D])}
    g_ffn = dt_("g_ffn", [DEPTH, D])
    g_final = dt_("g_final", [D])
    c_ident = dt_("c_ident", [128, 128])
    c_J = dt_("c_J", [128, 128])
    c_Uinc = dt_("c_Uinc", [128, 128])
    c_Ustr = dt_("c_Ustr", [128, 128])
    c_maskU = dt_("c_maskU", [128, 128])
    c_OH = dt_("c_OH", [32, 384])

    x_sample = dt_("x_sample", [32, D])
    cache_k = dt_("cache_k", [DEPTH, 2, 128, 128])
    cache_v = dt_("cache_v", [DEPTH, 2, 128, 128])
    state_conv = dt_("state_conv", [DEPTH, 6, 1536])
    state_ssm = dt_("state_ssm", [DEPTH, 2, 1024, 128])
    y_sample = dt_("y_sample", [32, D], kind="ExternalOutput")
    k_samp = dt_("k_sample", [DEPTH, 2, 128, 128], kind="ExternalOutput")
    v_samp = dt_("v_sample", [DEPTH, 2, 128, 128], kind="ExternalOutput")
    conv_samp = dt_("conv_sample", [DEPTH, 2, 3, 1536], kind="ExternalOutput")
    ssm_samp = dt_("ssm_sample", [DEPTH, 2, 1024, 128], kind="ExternalOutput")
    y_prompt = dt_("y_prompt", [seq, D], kind="ExternalOutput")
    k_out = dt_("k_prompt", [DEPTH, 128, 128], kind="ExternalOutput")
    v_out = dt_("v_prompt", [DEPTH, 128, 128], kind="ExternalOutput")
    conv_out = dt_("conv_prompt", [DEPTH, 3, 1536], kind="ExternalOutput")
    ssm_out = dt_("ssm_prompt", [DEPTH, 16 * 64, 128], kind="ExternalOutput")

    units = layer_units()
    uoff = {}
    off = 0
    for (name, wn, K, c0, n) in units:
        uoff[name] = off
        off += (K // 128) * n
    WSC_COLS = off
    wsc = [dt_(f"wsc{l}", [128, WSC_COLS], BF16, kind="Internal") for l in range(DEPTH)]
    bvd = dt_("bvd", [8, 384], F32, kind="Internal")

    es = ExitStack()
    with es:
        sems = [es.enter_context(nc.semaphore(f"e{i}")) for i in range(5)]
        S = Sched(nc, sems)
        nslot_ctr = [0]

        def newslot():
            nslot_ctr[0] += 1
            return Slot(es.enter_context(nc.semaphore(f"d{nslot_ctr[0]}")))

        def sb(name, shape, dtype=F32):
            return es.enter_context(nc.sbuf_tensor(name, list(shape), dtype))

        allslots = []

        def mkslot():
            s = newslot()
            allslots.append(s)
            return s

        x = sb("x", [128, KC, T])
        hn = sb("hn", [128, KC, T], BF16)
        zs = sb("zs", [128, KC, T], BF16)
        qT = sb("qT", [64, 8, T], BF16)
        oatt = sb("oatt", [128, 4, T], BF16)
        kd = [[sb(f"kd{l}{g}", [128, 128 + T], BF16) for g in range(2)] for l in range(DEPTH)]
        vt = [sb(f"vt{l}", [128, NSUB + 1, 128], BF16) for l in range(DEPTH)]
        big = sb("big", [128, 12800], BF16)
        hist = [sb(f"hist{l}", [128, 12, 3], BF16) for l in range(DEPTH)]
        ga = sb("ga", [128, KC, T], BF16)
        gs = sb("gs", [128, KC, T], BF16)
        y2 = sb("y2", [128, KC, T], BF16)
        wr = [sb(f"wr{i}", [128, SLOTW], BF16) for i in range(NSLOT)]
        stT = [sb(f"stT{l}", [128, 1024]) for l in range(DEPTH)]
        stB = [sb(f"stB{l}", [128, 1024], BF16) for l in range(DEPTH)]
        scr = sb("scr", [128, 4096])
        rhsh = scr[:, 0:1024].rearrange("p (h l) -> p h l", h=8)
        Esb = scr[:, 1024:2048].rearrange("p (h l) -> p h l", h=8)
        LT = scr[:, 2048:3072].rearrange("p (h l) -> p h l", h=8)
        sq = scr[:, 0:2048].bitcast(BF16).rearrange("p (k t) -> p k t", k=KC)
        mtmp = [scr[:, 3072:3584], scr[:, 3584:4096]]
        bvT = scr[0:8, 2048:2432]
        convT = scr[0:3, 2048:3584].rearrange("p (c q) -> p c q", c=12)
        Wt = sb("Wt", [128, 16, 128], BF16)
        CE = sb("CE", [128, 16, 128], BF16)
        xdt = sb("xdt", [128, 1024], BF16)
        xsc = sb("xsc", [128, 1024], BF16)
        Btok = sb("Btok", [128, 2, 128], BF16)
        CBm = sb("CBm", [128, 2, 128], BF16)
        pexp = sb("pexp", [128, 512])
        pT = [sb(f"pT{i}", [128, 512], BF16) for i in range(2)]
        rec = pexp
        expB = sb("expB", [128, 8, 2, 128], BF16)
        stg = sb("stg", [128, D])
        ctmp = [stg[:, 0:T], stg[:, T:2 * T]]
        rstd = sb("rstd", [128, T])
        stag = [stg, stg]
        SK = [("ctmp", 0), ("ctmp", 1)]
        ident = sb("ident", [128, 128])
        identb = sb("identb", [128, 128], BF16)
        Jm = sb("Jm", [128, 128])
        Uinc = sb("Uinc", [128, 128])
        Ustr = sb("Ustr", [128, 128])
        maskU = sb("maskU", [128, 128])
        onesb = sb("onesb", [128, 128], BF16)
        onesf = sb("onesf", [128, 128])
        hank = sb("hank", [128, 8, 2, 128])
        pstA = sb("pstA", [80, 128])
        pstB = sb("pstB", [96, 128])
        ptA = sb("ptA", [128, 80])
        ptB = sb("ptB", [128, 96])
        OHs = sb("OHs", [32, 384])
        tbs = sb("tbs", [32, 8])
        dfull = sb("dfull", [128, DEPTH, 16])
        sfull = sb("sfull", [128, DEPTH, 8])
        dtb = sb("dtb", [128, DEPTH, 16])
        abc = sb("abc", [128, DEPTH, 16])
        dsk = sb("dsk", [128, DEPTH, 8])
        diagD = sb("diagD", [128, DEPTH, 8, 128], BF16)
        esink = sb("esink", [128, DEPTH, 4])
        dtt = sb("dtt", [128, 16])
        dat = sb("dat", [128, 16])
        dtraw = sb("dtraw", [128, NSUB, 16])
        rh2 = sb("rh2", [128, 1024])
        decend = sb("decend", [128, 16])
        kvo = sb("kvo", [128, 256])
        convo = sb("convo", [128, 2, 12, 3])
        sso = scr[:, 2048:3072].rearrange("p (c n) -> p c n", c=8)
        print("SBUF remaining", nc.sbuf_bytes_remaining)
        pb = [es.enter_context(nc.psum_tensor(f"pb{i}", [128, 512], F32)) for i in range(8)]

        PB = lambda i: ("pb", i)

        def bigpages(lo, hi):
            return [("big", p) for p in range(lo // 512, (hi - 1) // 512 + 1)]

        def xbcp(c):
            return big[:, 515 * c:515 * c + 515]

        XC0 = 6400

        def xc(c):
            return big[:, XC0 + 512 * c:XC0 + 512 * c + 512]

        def actv(c):
            return big[:, 512 * c:512 * c + 512]

        kxbcp = lambda c: bigpages(515 * c, 515 * c + 515)
        kxc = lambda c: bigpages(XC0 + 512 * c, XC0 + 512 * c + 512)
        kact = lambda c: bigpages(512 * c, 512 * c + 512)

        ldslot = mkslot()
        def ld(dst, src, key, q="sp", **kw):
            S.dma(q, ldslot, dst, src, w=[key], batch=True, **kw)

        ld(ident[:], c_ident[:, :], "ident")
        ld(Jm[:], c_J[:, :], "Jm")
        ld(Uinc[:], c_Uinc[:, :], "Uinc")
        ld(Ustr[:], c_Ustr[:, :], "Ustr")
        ld(maskU[:], c_maskU[:, :], "maskU")
        NC_ = dict(allow_slow_non_contiguous=True)
        ld(pstA[0:16, :], g_mix.rearrange("l (k p) -> (l k) p", p=128), "pstA")
        ld(pstA[16:32, :], g_ffn.rearrange("l (k p) -> (l k) p", p=128), "pstA")
        ld(pstA[32:48, :], g_ssd.rearrange("l (k p) -> (l k) p", p=128), "pstA")
        ld(pstA[48:56, :], g_final.rearrange("(k p) -> k p", p=128), "pstA")
        ld(pstA[56:80, :], conv_b.rearrange("l (c p) -> (l c) p", p=128), "pstA")
        ld(pstB[:, :], conv_w.rearrange("l j (c p) -> (l j c) p", p=128), "pstB")
        ld(dtb[:], dt_bias.partition_broadcast(128), "dtb")
        ld(abc[:], a_log.partition_broadcast(128), "abc")
        ld(dfull[:], d_skip.partition_broadcast(128), "dfull")
        ld(sfull[:], sinks.partition_broadcast(128), "sfull")
        ld(OHs[:], c_OH[:, :], "OHs")
        ld(tbs[:], rel_table[:, :], "tbs")
        S.close_batch(ldslot)
        S.op("pe", OP("transpose", pb[0][:, 0:80], pstA[:, :], ident[0:80, 0:80]), r=["pstA", "ident"], w=[PB(0)])
        S.op("pe", OP("transpose", pb[0][:, 128:224], pstB[:, :], ident[0:96, 0:96]), r=["pstB", "ident"], w=[PB(0)])
        S.op("dve", OP("tensor_copy", out=ptA[:], in_=pb[0][:, 0:80]), r=[PB(0)], w=["ptA"])
        S.op("dve", OP("tensor_copy", out=ptB[:], in_=pb[0][:, 128:224]), r=[PB(0)], w=["ptB"])
        S.op("dve", OP("tensor_scalar", out=ptA[:, 0:56], in0=ptA[:, 0:56], scalar1=32.0, scalar2=None, op0=ALU.mult),
             r=["ptA"], w=["ptA"])
        for hh in range(2):
            pr = slice(hh * 64, hh * 64 + 64)
            S.op("dve", OP("tensor_copy", out=dsk[pr], in_=dfull[pr].rearrange("p l (j t) -> p l j t", t=2)[:, :, :, hh]),
                 r=["dfull"], w=["dsk"])
            S.op("dve", OP("tensor_copy", out=esink[pr], in_=sfull[pr].rearrange("p l (j t) -> p l j t", t=2)[:, :, :, hh]),
                 r=["sfull"], w=["esink"])
        S.op("dve", OP("memset", onesb[:], 1.0), w=["onesb"])
        S.op("dve", OP("memset", onesf[:], 1.0), w=["onesf"])
        S.op("dve", OP("tensor_copy", out=identb[:], in_=ident[:]), r=["ident"], w=["identb"])
        S.op("act", OP("activation", out=abc[:], in_=abc[:], func=AF.Exp), r=["abc"], w=["abc"])
        S.op("dve", OP("tensor_scalar", out=abc[:], in0=abc[:], scalar1=-1.0, scalar2=None, op0=ALU.mult),
             r=["abc"], w=["abc"])
        S.op("act", OP("activation", out=esink[:], in_=esink[:], func=AF.Exp), r=["esink"], w=["esink"])
        for l in range(DEPTH):
            for j in range(8):
                S.op("dve", OP("tensor_scalar", out=diagD[:, l, j, :], in0=ident[:], scalar1=dsk[:, l, j:j + 1],
                                                                scalar2=None, op0=ALU.mult),
                     r=["ident", "dsk"], w=["diagD"])
        for l in range(DEPTH):
            S.op("dve", OP("memset", stT[l][:], 0.0), w=[("stT", l)])
            S.op("dve", OP("memset", stB[l][:], 0.0), w=[("stB", l)])
            S.op("dve", OP("memset", hist[l][:], 0.0), w=[("hist", l)])

        bslot = mkslot()
        hslot = mkslot()
        S.op("pe", OP("matmul", pb[1][0:8, 0:384], lhsT=tbs[:, :], rhs=OHs[:, :], start=True, stop=True), r=["tbs", "OHs"], w=[PB(1)])
        S.op("dve", OP("tensor_copy", out=bvT, in_=pb[1][0:8, 0:384]), r=[PB(1)], w=[("scr", 2)])
        S.dma("sp", bslot, bvd[:, :], bvT, r=[("scr", 2)], w=["bvd"])
        for h in range(8):
            for blk in range(2):
                C = 128 * (1 - blk)
                src = bass.AP(bvd.tensor, h * 384 + C, [[1, 128], [1, 128]])
                S.dma("act", hslot, hank[:, h, blk, :], src, r=["bvd"], w=[("hank", h, blk)], batch=True)
        S.close_batch(hslot)
        for h in range(8):
            S.op("pe", OP("matmul", pb[h % 2][:, 0:256], lhsT=Jm[:], rhs=hank[:, h].rearrange("p b q -> p (b q)"),
                                               start=True, stop=True),
                 r=["Jm", ("hank", h, 0), ("hank", h, 1)], w=[PB(h % 2)])
            S.op("act", OP("activation", out=expB[:, h].rearrange("p b q -> p (b q)"), in_=pb[h % 2][:, 0:256],
                                                    func=AF.Exp),
                 r=[PB(h % 2)], w=["expB"])
        S.op("dve", OP("tensor_scalar", out=expB[0:64, :, 0, 64:128], in0=expB[0:64, :, 0, 64:128], scalar1=0.0, scalar2=None, op0=ALU.mult), r=["expB"], w=["expB"])
        S.op("dve", OP("tensor_scalar", out=expB[64:128, :, 1, 0:64], in0=expB[64:128, :, 1, 0:64], scalar1=0.0, scalar2=None, op0=ALU.mult), r=["expB"], w=["expB"])

        STAGE = int(os.environ.get('KSTAGE', '99'))
        cslots = [mkslot(), mkslot()]
        cctr = [0]

        def cslot_next():
            cctr[0] += 1
            return cslots[cctr[0] % 2]
        for l in (range(DEPTH) if STAGE >= 1 else []):
            for (name, wn, K, c0, n) in units:
                kc = K // 128
                o = uoff[name]
                dst = wsc[l][:, o:o + kc * n].rearrange("p (k c) -> p k c", k=kc)
                wl = W[wn][l]
                if name == "kv":
                    for g in range(2):
                        for rep in range(2):
                            S.dma("pool", cslot_next(), dst[:, :, (2 * g + rep) * 64:(2 * g + rep) * 64 + 64],
                                  wl[:, 512 + g * 64:512 + g * 64 + 64].rearrange("(k p) c -> p k c", p=128),
                                  w=[("wsc", l, name)], serial=True)
                    S.dma("pool", cslot_next(), dst[:, :, 256:512], wl[:, 512:768].rearrange("(k p) c -> p k c", p=128),
                          w=[("wsc", l, name)], serial=True)
                else:
                    S.dma("pool", cslot_next(), dst, wl[:, c0:c0 + n].rearrange("(k p) c -> p k c", p=128), w=[("wsc", l, name)], serial=True)

        wslots = [mkslot() for _ in range(NSLOT)]
        order = [(l, u) for _t in range(NT + 1) for l in range(DEPTH) for u in units]
        ws = {"issued": 0, "pos": 0}

        def wget(l, name, hold=0):
            i = ws["pos"]
            assert order[i][0] == l and order[i][1][0] == name, (order[i], l, name)
            while ws["issued"] < min(len(order), i + NSLOT - hold):
                j = ws["issued"]
                ll, (nm, wn, K, c0, n) = order[j]
                kc = K // 128
                o = uoff[nm]
                S.dma("sp", wslots[j % NSLOT], wr[j % NSLOT][:, 0:kc * n], wsc[ll][:, o:o + kc * n],
                      r=[("wsc", ll, nm)], w=[("wr", j % NSLOT)])
                ws["issued"] += 1
            ws["pos"] += 1
            _, (nm, wn, K, c0, n) = order[i]
            kc = K // 128
            return wr[i % NSLOT][:, 0:kc * n].rearrange("p (k c) -> p k c", k=kc), ("wr", i % NSLOT)

        rr = {"ev": 0, "pbd": 0}

        def dense_bank():
            rr["pbd"] ^= 1
            return rr["pbd"]

        def mm(out, lhsT, rhs, start, stop, r, w):
            S.op("pe", OP("matmul", out, lhsT=lhsT, rhs=rhs, start=start, stop=stop), r=r, w=w)

        xslot = [mkslot(), mkslot()]
        yslot = [mkslot(), mkslot()]
        kvslot = mkslot()
        cvslot = mkslot()
        ssslot = mkslot()
        kcslot = mkslot()

        def rsq(src, dst, skey):
            S.op("dve", OP("tensor_scalar", out=dst, in0=src, scalar1=1024.0 * EPS, scalar2=None, op0=ALU.add),
                 r=[skey], w=["rstd"])
            S.op("act", OP("activation", out=dst, in_=dst, func=AF.Ln), r=["rstd"], w=["rstd"])
            S.op("act", OP("activation", out=dst, in_=dst, func=AF.Exp, scale=-0.5), r=["rstd"], w=["rstd"])

        def rmsnorm_to(dst, dkeyf, gcol, n=T):
            for kc in range(KC):
                if kc % 2 == 0:
                    S.op("act", OP("activation", out=sq[:, kc, 0:n], in_=x[:, kc, 0:n], func=AF.Square),
                         r=[("x", kc)], w=[("scr", 0), ("scr", 1)])
                else:
                    S.op("dve", OP("tensor_tensor", out=sq[:, kc, 0:n], in0=x[:, kc, 0:n], in1=x[:, kc, 0:n], op=ALU.mult),
                         r=[("x", kc)], w=[("scr", 0), ("scr", 1)])
            b = 2
            for kc in range(KC):
                mm(pb[b][:, 0:n], onesb[:], sq[:, kc, 0:n], kc == 0, kc == KC - 1, ["onesb", ("scr", 0), ("scr", 1)], [PB(b)])
            rsq(pb[b][:, 0:n], rstd[:, 0:n], PB(b))
            for kc in range(KC):
                eng = "dve"
                S.op(eng, OP("scalar_tensor_tensor", out=dst[:, kc, 0:n], in0=x[:, kc, 0:n], scalar=gcol(kc),
                                                                  in1=rstd[:, 0:n], op0=ALU.mult, op1=ALU.mult),
                     r=[("x", kc), "rstd", "ptA"], w=[dkeyf(kc)])

        def dense_oc(wv, wkey, ocl, nk, rhs_of, rkeys, n=T):
            b = dense_bank()
            for kc in range(nk):
                mm(pb[b][:, 0:n], wv[:, kc, ocl * 128:(ocl + 1) * 128], rhs_of(kc), kc == 0, kc == nk - 1,
                   [wkey] + rkeys(kc), [PB(b)])
            return b

        def evac_copy(dst, b, wkeys, n=T):
            rr["ev"] ^= 1
            if rr["ev"]:
                S.op("act", OP("activation", out=dst, in_=pb[b][:, 0:n], func=AF.Copy), r=[PB(b)], w=wkeys)
            else:
                S.op("dve", OP("tensor_copy", out=dst, in_=pb[b][:, 0:n]), r=[PB(b)], w=wkeys)

        def evac_act(dst, b, func, wkeys, n=T):
            S.op("act", OP("activation", out=dst, in_=pb[b][:, 0:n], func=func), r=[PB(b)], w=wkeys)

        hn_rhs = lambda kc: hn[:, kc, :]
        hn_keys = lambda kc: [("hn", kc)]

        def load_tokens(src_rows_ap, rows, col0, slot):
            S.dma("sp", slot, stg[0:rows, :], src_rows_ap, w=SK)
            for half in range(2):
                b = 4 + half
                for c4 in range(4):
                    kc = half * 4 + c4
                    S.op("pe", OP("transpose", pb[b][:, c4 * rows:(c4 + 1) * rows], stg[0:rows, kc * 128:(kc + 1) * 128],
                                  ident[0:rows, 0:rows]), r=SK + ["ident"], w=[PB(b)])
                dstv = x[:, half * 4:half * 4 + 4, col0:col0 + rows]
                srcv = pb[b][:, 0:4 * rows].rearrange("p (c t) -> p c t", c=4)
                wk_ = [("x", half * 4 + c) for c in range(4)]
                if half == 0:
                    S.op("dve", OP("tensor_copy", out=dstv, in_=srcv), r=[PB(b)], w=wk_)
                else:
                    S.op("act", OP("activation", out=dstv, in_=srcv, func=AF.Copy), r=[PB(b)], w=wk_)

        def store_tokens(dst_rows_ap, rows, col0, slot):
            for half in range(2):
                b = 4 + half
                for c4 in range(4):
                    kc = half * 4 + c4
                    S.op("pe", OP("transpose", pb[b][0:rows, c4 * 128:(c4 + 1) * 128], x[:, kc, col0:col0 + rows], ident[:]),
                         r=[("x", kc), "ident"], w=[PB(b)])
                if half == 0:
                    S.op("dve", OP("tensor_copy", out=stg[0:rows, 0:512], in_=pb[b][0:rows, :]), r=[PB(b)], w=SK)
                else:
                    S.op("act", OP("activation", out=stg[0:rows, 512:1024], in_=pb[b][0:rows, :], func=AF.Copy), r=[PB(b)], w=SK)
            S.dma("pool", slot, dst_rows_ap, stg[0:rows, :], r=SK)

        def final_norm(n):
            for kc in range(KC):
                S.op("act", OP("activation", out=sq[:, kc, 0:n], in_=x[:, kc, 0:n], func=AF.Square),
                     r=[("x", kc)], w=[("scr", 0), ("scr", 1)])
            for kc in range(KC):
                mm(pb[2][:, 0:n], onesb[:], sq[:, kc, 0:n], kc == 0, kc == KC - 1, ["onesb", ("scr", 0), ("scr", 1)], [PB(2)])
            rsq(pb[2][:, 0:n], rstd[:, 0:n], PB(2))
            for kc in range(KC):
                S.op("dve", OP("scalar_tensor_tensor", out=x[:, kc, 0:n], in0=x[:, kc, 0:n], scalar=ptA[:, 48 + kc:48 + kc + 1],
                               in1=rstd[:, 0:n], op0=ALU.mult, op1=ALU.mult),
                     r=[("x", kc), "rstd", "ptA"], w=[("x", kc)])

        def v3(base, parts, R):
            return scr[0:parts, base:base + 8 * R].rearrange("p (h l) -> p h l", h=8)

        def ssd_block(l, R, cs, dti):
            rows = slice(0, R)
            rh, Ev, LTv = v3(0, R, R), v3(1024, 128, R), v3(2048, R, R)
            HPM = min(8, 512 // R)
            S.op("act", OP("activation", out=dtt[rows, :], in_=dtraw[rows, dti, :], func=AF.Exp), r=["dtraw"], w=["dtt"])
            S.op("act", OP("activation", out=dtt[rows, :], in_=dtt[rows, :], func=AF.Ln, bias=1.0), r=["dtt"], w=["dtt"])
            S.op("dve", OP("tensor_tensor", out=dat[rows, :], in0=dtt[rows, :], in1=abc[rows, l, :], op=ALU.mult),
                 r=["dtt", "abc"], w=["dat"])
            scr2b = scr[:, 2048:3072].bitcast(BF16)
            scr1b = scr[:, 1024:2048].bitcast(BF16)
            rhs_ = [rh, rh2[0:R, 0:8 * R].rearrange("p (h l) -> p h l", h=8)]
            rhk = [("scr", 0), "rh2"]
            for g in range(2):
                hs = slice(g * 8, g * 8 + 8)
                S.op("dve", OP("tensor_tensor", out=rhs_[g], in0=Uinc[rows, 0:R].unsqueeze(1).to_broadcast([R, 8, R]),
                               in1=dat[rows, hs].unsqueeze(2).to_broadcast([R, 8, R]), op=ALU.mult),
                     r=["Uinc", "dat"], w=[rhk[g]])
            pxt = pb[4][:, :].bitcast(BF16)
            for j in range(8):
                S.op("pe", OP("transpose", pxt[rows, j * 128:(j + 1) * 128], xc(j)[:, cs], identb[:]),
                     r=kxc(j) + ["identb"], w=[PB(4)])
            pbt = pb[5][:, :].bitcast(BF16)
            for g in range(2):
                S.op("pe", OP("transpose", pbt[rows, g * 128:(g + 1) * 128], xc(8 + g)[:, cs], identb[:]),
                     r=kxc(8 + g) + ["identb"], w=[PB(5)])
            S.op("act", OP("activation", out=Btok[rows].rearrange("p g n -> p (g n)"), in_=pbt[rows, 0:256], func=AF.Copy),
                 r=[PB(5)], w=["Btok"])
            S.op("dve", OP("tensor_tensor", out=xdt[rows].rearrange("p (h d) -> p h d", h=16),
                           in0=pxt[rows, :].rearrange("p (h d) -> p h d", h=16),
                           in1=dtt[rows, :].unsqueeze(2).to_broadcast([R, 16, 64]), op=ALU.mult),
                 r=[PB(4), "dtt"], w=["xdt"])
            for g in range(2):
                mm(pb[5][rows, 256 + g * R:256 + (g + 1) * R], xc(8 + g)[:, cs], xc(10 + g)[:, cs], True, True,
                   kxc(8 + g) + kxc(10 + g), [PB(5)])
            S.op("dve", OP("tensor_tensor", out=CBm[rows, :, 0:R], in0=pb[5][rows, 256:256 + 2 * R].rearrange("p (g l) -> p g l", g=2),
                           in1=maskU[rows, 0:R].unsqueeze(1).to_broadcast([R, 2, R]), op=ALU.mult),
                 r=[PB(5), "maskU"], w=["CBm"])
            LTg = [scr2b[0:R, g * 1024:g * 1024 + 8 * R].rearrange("p (h l) -> p h l", h=8) for g in range(2)]
            Eg = [scr1b[:, g * 1024:g * 1024 + 8 * R].rearrange("p (h l) -> p h l", h=8) for g in range(2)]
            bank = [[2, 3], [6, 7]]
            for g in range(2):
                for half in range(8 // HPM):
                    hsl = slice(half * HPM, (half + 1) * HPM)
                    bk = bank[g][half]
                    mm(pb[bk][rows, 0:HPM * R], Ustr[rows, 0:R], rhs_[g][:, hsl, :].rearrange("p h l -> p (h l)"), True, True,
                       ["Ustr", rhk[g]], [PB(bk)])
                    S.op("act", OP("activation", out=LTg[g][:, hsl, :].rearrange("p h l -> p (h l)"), in_=pb[bk][rows, 0:HPM * R],
                                   func=AF.Exp), r=[PB(bk)], w=[("scr", 2)])
            for g in range(2):
                for half in range(8 // HPM):
                    hsl = slice(half * HPM, (half + 1) * HPM)
                    bk = bank[g][half]
                    mm(pb[bk][:, 0:HPM * R], onesf[rows, :], rhs_[g][:, hsl, :].rearrange("p h l -> p (h l)"), True, True,
                       ["onesf", rhk[g]], [PB(bk)])
                    S.op("act", OP("activation", out=Eg[g][:, hsl, :].rearrange("p h l -> p (h l)"), in_=pb[bk][:, 0:HPM * R],
                                   func=AF.Exp), r=[PB(bk)], w=[("scr", 1)])
            for g in range(2):
                hs = slice(g * 8, g * 8 + 8)
                S.op("dve", OP("tensor_tensor", out=Wt[rows, hs, 0:R], in0=LTg[g],
                               in1=CBm[rows, g, 0:R].unsqueeze(1).to_broadcast([R, 8, R]), op=ALU.mult),
                     r=[("scr", 2), "CBm"], w=[("Wt", g)])
                S.op("dve", OP("tensor_tensor", out=CE[:, hs, 0:R], in0=Eg[g],
                               in1=xc(10 + g)[:, cs].unsqueeze(1).to_broadcast([128, 8, R]), op=ALU.mult),
                     r=[("scr", 1)] + kxc(10 + g), w=[("CE", g)])
                S.op("dve", OP("tensor_copy", out=decend[rows, hs], in_=LTg[g][:, :, R - 1]), r=[("scr", 2)], w=[("decend", g)])
                S.op("dve", OP("tensor_tensor", out=stT[l][:, g * 512:(g + 1) * 512].rearrange("p (h d) -> p h d", h=8),
                               in0=stT[l][:, g * 512:(g + 1) * 512].rearrange("p (h d) -> p h d", h=8),
                               in1=Eg[g][:, :, R - 1:R].to_broadcast([128, 8, 64]), op=ALU.mult),
                     r=[("scr", 1), ("stT", l)], w=[("stT", l)])
            S.op("dve", OP("tensor_tensor", out=xsc[rows].rearrange("p (h d) -> p h d", h=16),
                           in0=xdt[rows].rearrange("p (h d) -> p h d", h=16),
                           in1=decend[rows, :].unsqueeze(2).to_broadcast([R, 16, 64]), op=ALU.mult),
                 r=["xdt", ("decend", 0), ("decend", 1)], w=["xsc"])
            for j in range(8):
                b = j // 4
                col = slice((j % 4) * R, (j % 4 + 1) * R)
                g = j // 4
                mm(pb[b][:, col], diagD[:, l, j, :], xc(j)[:, cs], True, False, ["diagD"] + kxc(j), [PB(b)])
                for hh in range(2):
                    h = 2 * j + hh
                    prt = slice(hh * 64, hh * 64 + 64)
                    mm(pb[b][prt, col], xdt[rows, h * 64:(h + 1) * 64], Wt[rows, h, 0:R], False, False, ["xdt", ("Wt", g)], [PB(b)])
                    mm(pb[b][prt, col], stB[l][:, h * 64:(h + 1) * 64], CE[:, h, 0:R], False, True, [("stB", l), ("CE", g)], [PB(b)])
            for b in range(2):
                S.op("dve", OP("tensor_tensor", out=y2[:, b * 4:b * 4 + 4, cs],
                               in0=pb[b][:, 0:4 * R].rearrange("p (j t) -> p j t", j=4),
                               in1=zs[:, b * 4:b * 4 + 4, cs], op=ALU.mult),
                     r=[PB(b)] + [("zs", b * 4 + j) for j in range(4)], w=[("y2", b * 4 + j) for j in range(4)])
            for g in range(2):
                mm(pb[2 + g][:, :], Btok[rows, g, :], xsc[rows, g * 512:(g + 1) * 512], True, True, ["Btok", "xsc"], [PB(2 + g)])
                S.op("dve", OP("tensor_tensor", out=stT[l][:, g * 512:(g + 1) * 512], in0=stT[l][:, g * 512:(g + 1) * 512],
                               in1=pb[2 + g][:, :], op=ALU.add), r=[PB(2 + g), ("stT", l)], w=[("stT", l)])
            S.op("act", OP("activation", out=stB[l][:], in_=stT[l][:], func=AF.Copy), r=[("stT", l)], w=[("stB", l)])

        def state_out(l, dst):
            for c in range(8):
                b = 2 + c // 4
                S.op("pe", OP("transpose", pb[b][:, (c % 4) * 128:(c % 4 + 1) * 128], stT[l][:, c * 128:(c + 1) * 128], ident[:]),
                     r=[("stT", l), "ident"], w=[PB(b)])
            for b in range(2):
                S.op("dve", OP("tensor_copy", out=sso[:, b * 4:b * 4 + 4, :], in_=pb[2 + b][:, :].rearrange("p (c n) -> p c n", c=4)),
                     r=[PB(2 + b)], w=[("scr", 2)])
            S.dma("pool", ssslot, dst.rearrange("(c p) n -> p c n", p=128), sso, r=[("scr", 2)])

        def conv_out_store(l, cvi, dst):
            for c in range(12):
                S.op("pe", OP("transpose", pb[3][0:3, c * 128 % 512:c * 128 % 512 + 128], convo[:, cvi, c, :], ident[:]),
                     r=["convo", "ident"], w=[PB(3)])
                if c % 4 == 3:
                    S.op("dve", OP("tensor_copy", out=convT[:, c - 3:c + 1, :], in_=pb[3][0:3, :].rearrange("p (c q) -> p c q", c=4)),
                         r=[PB(3)], w=[("scr", 2), "mtmp0"])
            S.dma("pool", cvslot, dst.rearrange("j (c p) -> j c p", p=128), convT, r=[("scr", 2), "mtmp0"])

        def attn_finish(l, ocs, nq):
            rv = rec[:, 0:4 * nq].rearrange("p (j q) -> p j q", j=4)
            S.op("dve", OP("tensor_tensor", out=rv, in0=pb[7][:, 0:4 * nq].rearrange("p (j q) -> p j q", j=4),
                           in1=esink[:, l, :].unsqueeze(2).to_broadcast([128, 4, nq]), op=ALU.add),
                 r=[PB(7), "esink"], w=["pexp"])
            S.op("dve", OP("reciprocal", out=rec[:, 0:4 * nq], in_=rec[:, 0:4 * nq]), r=["pexp"], w=["pexp"])
            S.op("dve", OP("tensor_tensor", out=oatt[:, :, ocs], in0=pb[6][:, 0:4 * nq].rearrange("p (j q) -> p j q", j=4),
                           in1=rv, op=ALU.mult), r=[PB(6), "pexp"], w=[("oatt", j) for j in range(4)])

        def attn_prompt(l, s, gt):
            cs = slice(s * 128, (s + 1) * 128)
            blks = [1] if gt == 0 else [0, 1]

            def st(hp):
                g = hp // 2
                b = 4 + hp % 2
                for hh in range(2):
                    for blk in blks:
                        kcol = slice(s * 128 + blk * 128, s * 128 + blk * 128 + 128)
                        mm(pb[b][:, (hh * 2 + blk) * 128:(hh * 2 + blk + 1) * 128], kd[l][g][0:64, kcol], qT[:, 2 * hp + hh, cs],
                           True, True, [("kd", l, g), ("qT", 2 * hp + hh)], [PB(b)])

            def em(hp):
                b = 4 + hp % 2
                if gt == 0:
                    for hh in range(2):
                        cc = slice((hh * 2 + 1) * 128, (hh * 2 + 2) * 128)
                        S.op("act", OP("activation", out=pexp[:, cc], in_=pb[b][:, cc], func=AF.Exp, scale=0.125), r=[PB(b)], w=["pexp"])
                        S.op("dve", OP("tensor_tensor", out=pT[hp % 2][:, cc], in0=pexp[:, cc], in1=expB[:, 2 * hp + hh, 1, :], op=ALU.mult),
                             r=["pexp", "expB"], w=[("pT", hp % 2)])
                else:
                    S.op("act", OP("activation", out=pexp[:], in_=pb[b][:, :], func=AF.Exp, scale=0.125), r=[PB(b)], w=["pexp"])
                    S.op("dve", OP("tensor_tensor", out=pT[hp % 2][:], in0=pexp[:],
                                   in1=expB[:, 2 * hp:2 * hp + 2].rearrange("p h b q -> p (h b q)"), op=ALU.mult),
                         r=["pexp", "expB"], w=[("pT", hp % 2)])

            def pv(hp):
                g = hp // 2
                for hh in range(2):
                    prt = slice(hh * 64, hh * 64 + 64)
                    for bi, blk in enumerate(blks):
                        pcol = slice((hh * 2 + blk) * 128, (hh * 2 + blk + 1) * 128)
                        mm(pb[6][prt, hp * 128:(hp + 1) * 128], vt[l][:, s + blk, g * 64:(g + 1) * 64], pT[hp % 2][:, pcol],
                           bi == 0, bi == len(blks) - 1, [("vt", l, s + blk), ("pT", hp % 2)], [PB(6)])
                    for bi, blk in enumerate(blks):
                        pcol = slice((hh * 2 + blk) * 128, (hh * 2 + blk + 1) * 128)
                        mm(pb[7][prt, hp * 128:(hp + 1) * 128], onesb[:, 0:64], pT[hp % 2][:, pcol],
                           bi == 0, bi == len(blks) - 1, ["onesb", ("pT", hp % 2)], [PB(7)])

            st(0)
            st(1)
            for hp in range(4):
                em(hp)
                pv(hp)
                if hp + 2 < 4:
                    st(hp + 2)
            attn_finish(l, cs, 128)

        def attn_sample(l, bq):
            qs = slice(bq * 16, bq * 16 + 16)
            S.dma("sp", xslot[0], stg[:, 0:128], cache_k[l, bq], w=SK)
            S.dma("sp", xslot[1], stg[:, 128:256], cache_v[l, bq], w=SK)
            for g in range(2):
                S.op("pe", OP("transpose", pb[4][0:64, g * 128:(g + 1) * 128], stg[:, g * 64:(g + 1) * 64], ident[:]),
                     r=SK + ["ident"], w=[PB(4)])
                S.op("dve", OP("tensor_copy", out=kd[l][g][0:64, 0:128], in_=pb[4][0:64, g * 128:(g + 1) * 128]), r=[PB(4)],
                     w=[("kd", l, g)])
            S.op("dve", OP("tensor_copy", out=vt[l][:, 0, :], in_=stg[:, 128:256]), r=SK, w=[("vt", l, 0)])
            S.dma("pool", kcslot, k_samp[l, bq, 0:112, :], cache_k[l, bq, 16:128, :])
            S.dma("pool", kcslot, v_samp[l, bq, 0:112, :], cache_v[l, bq, 16:128, :])
            for h in range(8):
                g = h // 4
                mm(pb[5][:, h * 32:h * 32 + 16], kd[l][g][0:64, 0:128], qT[:, h, qs], True, True, [("kd", l, g), ("qT", h)], [PB(5)])
                mm(pb[5][0:16, h * 32 + 16:h * 32 + 32], kd[l][g][0:64, 128 + bq * 16:128 + bq * 16 + 16], qT[:, h, qs], True, True,
                   [("kd", l, g), ("qT", h)], [PB(5)])
            pe3 = pexp[:, 0:256].rearrange("p (h c) -> p h c", h=8)
            ps3 = pb[5][:, 0:256].rearrange("p (h c) -> p h c", h=8)
            pt3 = pT[0][:, 0:256].rearrange("p (h c) -> p h c", h=8)
            S.op("act", OP("activation", out=pe3[:, :, 0:16], in_=ps3[:, :, 0:16], func=AF.Exp, scale=0.125), r=[PB(5)], w=["pexp"])
            S.op("act", OP("activation", out=pe3[0:16, :, 16:32], in_=ps3[0:16, :, 16:32], func=AF.Exp, scale=0.125), r=[PB(5)], w=["pexp"])
            S.op("dve", OP("tensor_tensor", out=pt3[:, :, 0:16], in0=pe3[:, :, 0:16], in1=expB[:, :, 0, 0:16], op=ALU.mult),
                 r=["pexp", "expB"], w=[("pT", 0)])
            S.op("dve", OP("tensor_tensor", out=pt3[0:16, :, 16:32], in0=pe3[0:16, :, 16:32], in1=expB[0:16, :, 1, 0:16], op=ALU.mult),
                 r=["pexp", "expB"], w=[("pT", 0)])
            for h in range(8):
                g, hp, hh = h // 4, h // 2, h % 2
                prt = slice(hh * 64, hh * 64 + 64)
                oc_ = slice(hp * 16, hp * 16 + 16)
                mm(pb[6][prt, oc_], vt[l][:, 0, g * 64:(g + 1) * 64], pt3[:, h, 0:16], True, False, [("vt", l, 0), ("pT", 0)], [PB(6)])
                mm(pb[6][prt, oc_], vt[l][0:16, 1 + bq, g * 64:(g + 1) * 64], pt3[0:16, h, 16:32], False, True,
                   [("vt", l, 1 + bq), ("pT", 0)], [PB(6)])
                mm(pb[7][prt, oc_], onesb[:, 0:64], pt3[:, h, 0:16], True, False, ["onesb", ("pT", 0)], [PB(7)])
                mm(pb[7][prt, oc_], onesb[0:16, 0:64], pt3[0:16, h, 16:32], False, True, ["onesb", ("pT", 0)], [PB(7)])
            attn_finish(l, qs, 16)

        def layer(l, mode, ti):
            prompt = mode == "prompt"
            n = T if prompt else 32
            last = prompt and ti == NT - 1
            hn_rhs = lambda kc: hn[:, kc, 0:n]
            hn_keys = lambda kc: [("hn", kc)]
            rmsnorm_to(hn, lambda kc: ("hn", kc), lambda kc: ptA[:, l * 8 + kc:l * 8 + kc + 1], n)
            wv, wk = wget(l, "q")
            for h in range(8):
                b = dense_bank()
                for kc in range(KC):
                    mm(pb[b][0:64, 0:n], wv[:, kc, h * 64:(h + 1) * 64], hn[:, kc, 0:n], kc == 0, kc == KC - 1, [wk, ("hn", kc)], [PB(b)])
                if h % 2 == 0:
                    S.op("act", OP("activation", out=qT[:, h, 0:n], in_=pb[b][0:64, 0:n], func=AF.Copy), r=[PB(b)], w=[("qT", h)])
                else:
                    S.op("dve", OP("tensor_copy", out=qT[:, h, 0:n], in_=pb[b][0:64, 0:n]), r=[PB(b)], w=[("qT", h)])
            wv, wk = wget(l, "kv")
            for g in range(2):
                b = dense_bank()
                for kc in range(KC):
                    mm(pb[b][0:64, 0:n], wv[:, kc, g * 128:g * 128 + 64], hn[:, kc, 0:n], kc == 0, kc == KC - 1, [wk, ("hn", kc)], [PB(b)])
                S.op("dve", OP("tensor_copy", out=kd[l][g][0:64, 128:128 + n], in_=pb[b][0:64, 0:n]), r=[PB(b)], w=[("kd", l, g)])
            if prompt:
                for s in range(NSUB):
                    for kc in range(KC):
                        mm(pb[6][:, 0:256], hn[:, kc, s * 128:(s + 1) * 128], wv[:, kc, 256:512], kc == 0, kc == KC - 1,
                           [wk, ("hn", kc)], [PB(6)])
                    S.op("dve", OP("tensor_copy", out=vt[l][:, s + 1, :], in_=pb[6][:, 128:256]), r=[PB(6)], w=[("vt", l, s + 1)])
                    if last and s == NSUB - 1:
                        S.op("dve", OP("tensor_copy", out=kvo[:], in_=pb[6][:, 0:256]), r=[PB(6)], w=["kvo"])
                        S.dma("pool", kvslot, k_out[l], kvo[:, 0:128], r=["kvo"])
                        S.dma("pool", kvslot, v_out[l], kvo[:, 128:256], r=["kvo"])
            else:
                for bq in range(2):
                    for kc in range(KC):
                        mm(pb[6][0:16, 0:256], hn[:, kc, bq * 16:(bq + 1) * 16], wv[:, kc, 256:512], kc == 0, kc == KC - 1,
                           [wk, ("hn", kc)], [PB(6)])
                    S.op("dve", OP("tensor_copy", out=vt[l][0:16, 1 + bq, :], in_=pb[6][0:16, 128:256]), r=[PB(6)], w=[("vt", l, 1 + bq)])
                    S.op("dve", OP("tensor_copy", out=kvo[0:16, :], in_=pb[6][0:16, 0:256]), r=[PB(6)], w=["kvo"])
                    S.dma("pool", kvslot, k_samp[l, bq, 112:128, :], kvo[0:16, 0:128], r=["kvo"])
                    S.dma("pool", kvslot, v_samp[l, bq, 112:128, :], kvo[0:16, 128:256], r=["kvo"])
            for zi in range(2):
                wv, wk = wget(l, f"z{zi}")
                for oc in range(4):
                    b = dense_oc(wv, wk, oc, KC, hn_rhs, hn_keys, n)
                    evac_act(zs[:, zi * 4 + oc, 0:n], b, AF.Silu, [("zs", zi * 4 + oc)], n)
            if prompt:
                for c in range(12):
                    S.op("dve", OP("tensor_copy", out=xbcp(c)[:, 0:3], in_=hist[l][:, c, :]), r=[("hist", l)], w=kxbcp(c))
            else:
                S.dma("sp", xslot[0], stg[0:72, 0:128], state_conv[l].rearrange("r (c p) -> (r c) p", p=128), w=SK)
                S.op("pe", OP("transpose", pb[4][:, 0:72], stg[0:72, 0:128], ident[0:72, 0:72]), r=SK + ["ident"], w=[PB(4)])
                hv = pb[4][:, 0:72].rearrange("p (b j c) -> p b j c", b=2, j=3)
                for c in range(12):
                    S.op("dve", OP("tensor_copy", out=xbcp(c)[:, 0:38].rearrange("p (b t) -> p b t", b=2)[:, :, 0:3], in_=hv[:, :, :, c]),
                         r=[PB(4)], w=kxbcp(c))
            for xi, nm in enumerate(("xs0", "xs1", "bc")):
                wv, wk = wget(l, nm)
                for oc in range(4):
                    c = xi * 4 + oc
                    b = dense_oc(wv, wk, oc, KC, hn_rhs, hn_keys, n)
                    if prompt:
                        evac_copy(xbcp(c)[:, 3:515], b, kxbcp(c))
                        if last:
                            S.op("dve", OP("tensor_copy", out=convo[:, 0, c, :], in_=pb[b][:, T - 3:T]), r=[PB(b)], w=["convo"])
                    else:
                        S.op("dve", OP("tensor_copy", out=xbcp(c)[:, 0:38].rearrange("p (b t) -> p b t", b=2)[:, :, 3:19],
                                       in_=pb[b][:, 0:32].rearrange("p (b t) -> p b t", b=2)), r=[PB(b)], w=kxbcp(c))
                        S.op("act", OP("activation", out=convo[:, :, c, :], in_=pb[b][:, 0:32].rearrange("p (b t) -> p b t", b=2)[:, :, 13:16],
                                       func=AF.Copy), r=[PB(b)], w=["convo"])
            if last:
                conv_out_store(l, 0, conv_out[l])
            if prompt:
                for c in range(12):
                    S.op("dve", OP("tensor_copy", out=hist[l][:, c, :], in_=xbcp(c)[:, T:T + 3]), r=kxbcp(c), w=[("hist", l)])
            else:
                for bq in range(2):
                    conv_out_store(l, bq, conv_samp[l, bq])
            wdt, wdk = wget(l, "dt")
            if prompt:
                for s in range(NSUB):
                    for kc in range(KC):
                        mm(pb[7][:, s * 16:(s + 1) * 16], hn[:, kc, s * 128:(s + 1) * 128], wdt[:, kc, 0:16], kc == 0, kc == KC - 1,
                           [wdk, ("hn", kc)], [PB(7)])
                S.op("dve", OP("tensor_tensor", out=dtraw[:], in0=pb[7][:, 0:64].rearrange("p (s h) -> p s h", s=NSUB),
                               in1=dtb[:, l, :].unsqueeze(1).to_broadcast([128, NSUB, 16]), op=ALU.add), r=[PB(7), "dtb"], w=["dtraw"])
            else:
                for bq in range(2):
                    for kc in range(KC):
                        mm(pb[7][0:16, bq * 16:(bq + 1) * 16], hn[:, kc, bq * 16:(bq + 1) * 16], wdt[:, kc, 0:16], kc == 0, kc == KC - 1,
                           [wdk, ("hn", kc)], [PB(7)])
                S.op("dve", OP("tensor_tensor", out=dtraw[0:16, 0:2, :], in0=pb[7][0:16, 0:32].rearrange("p (s h) -> p s h", s=2),
                               in1=dtb[0:16, l, :].unsqueeze(1).to_broadcast([16, 2, 16]), op=ALU.add), r=[PB(7), "dtb"], w=["dtraw"])
            cn = n if prompt else 35
            def conv_chunk(c):
                tmp = ctmp[c % 2]
                S.op("act", OP("activation", out=tmp[:, 0:cn], in_=xbcp(c)[:, 0:cn], func=AF.Copy, scale=ptB[:, l * 48 + c:l * 48 + c + 1]),
                     r=kxbcp(c) + ["ptB"], w=[("ctmp", c % 2)])
                for j in range(1, 4):
                    S.op("dve", OP("scalar_tensor_tensor", out=tmp[:, 0:cn], in0=xbcp(c)[:, j:j + cn],
                                   scalar=ptB[:, l * 48 + j * 12 + c:l * 48 + j * 12 + c + 1], in1=tmp[:, 0:cn],
                                   op0=ALU.mult, op1=ALU.add), r=kxbcp(c) + ["ptB", ("ctmp", c % 2)], w=[("ctmp", c % 2)])
                cbk = ptA[:, 56 + l * 12 + c:56 + l * 12 + c + 1]
                segs = [(slice(0, T), slice(0, T))] if prompt else [(slice(0, 16), slice(0, 16)), (slice(16, 32), slice(19, 35))]
                for od, sd in segs:
                    S.op("act", OP("activation", out=xc(c)[:, od], in_=tmp[:, sd], func=AF.Sigmoid, bias=cbk),
                         r=[("ctmp", c % 2), "ptA"], w=kxc(c))
                for od, sd in segs:
                    S.op("dve", OP("scalar_tensor_tensor", out=xc(c)[:, od], in0=tmp[:, sd], scalar=cbk, in1=xc(c)[:, od],
                                   op0=ALU.add, op1=ALU.mult), r=[("ctmp", c % 2), "ptA"] + kxc(c), w=kxc(c))
            cq = list(range(12))
            for gt_, gname in ((ga, "ga"), (gs, "gs")):
                for hi in range(2):
                    wv, wk = wget(l, f"{gname}{hi}")
                    for oc in range(4):
                        b = dense_oc(wv, wk, oc, KC, hn_rhs, hn_keys, n)
                        evac_act(gt_[:, hi * 4 + oc, 0:n], b, AF.Sigmoid, [(gname, hi * 4 + oc)], n)
                        if cq:
                            conv_chunk(cq.pop(0))
            while cq:
                conv_chunk(cq.pop(0))
            if prompt:
                for s in range(NSUB):
                    attn_prompt(l, s, ti * NSUB + s)
                    ssd_block(l, 128, slice(s * 128, (s + 1) * 128), s)
                if not last:
                    for g in range(2):
                        S.op("dve", OP("tensor_copy", out=kd[l][g][0:64, 0:128], in_=kd[l][g][0:64, T:T + 128]), r=[("kd", l, g)],
                             w=[("kd", l, g)])
                    S.op("dve", OP("tensor_copy", out=vt[l][:, 0, :], in_=vt[l][:, NSUB, :]), r=[("vt", l, NSUB)], w=[("vt", l, 0)])
                else:
                    state_out(l, ssm_out[l])
            else:
                for bq in range(2):
                    attn_sample(l, bq)
                    S.dma("sp", xslot[0], stg[:, :].rearrange("p (c n) -> p c n", c=8),
                          state_ssm[l, bq].rearrange("(c p) n -> p c n", p=128), w=SK)
                    for c in range(8):
                        b = 2 + c // 4
                        S.op("pe", OP("transpose", pb[b][:, (c % 4) * 128:(c % 4 + 1) * 128], stg[:, c * 128:(c + 1) * 128], ident[:]),
                             r=SK + ["ident"], w=[PB(b)])
                    for b in range(2):
                        S.op("dve", OP("tensor_copy", out=stT[l][:, b * 512:(b + 1) * 512], in_=pb[2 + b][:, :]), r=[PB(2 + b)],
                             w=[("stT", l)])
                    S.op("act", OP("activation", out=stB[l][:], in_=stT[l][:], func=AF.Copy), r=[("stT", l)], w=[("stB", l)])
                    ssd_block(l, 16, slice(bq * 16, bq * 16 + 16), bq)
                    state_out(l, ssm_samp[l, bq])
            for kc in range(KC):
                S.op("act", OP("activation", out=sq[:, kc, 0:n], in_=y2[:, kc, 0:n], func=AF.Square), r=[("y2", kc)],
                     w=[("scr", 0), ("scr", 1)])
            for kc in range(KC):
                mm(pb[2][:, 0:n], onesb[:], sq[:, kc, 0:n], kc == 0, kc == KC - 1, ["onesb", ("scr", 0), ("scr", 1)], [PB(2)])
            rsq(pb[2][:, 0:n], rstd[:, 0:n], PB(2))
            for kc in range(KC):
                S.op("dve", OP("scalar_tensor_tensor", out=hn[:, kc, 0:n], in0=y2[:, kc, 0:n],
                               scalar=ptA[:, 32 + l * 8 + kc:32 + l * 8 + kc + 1], in1=rstd[:, 0:n], op0=ALU.mult, op1=ALU.mult),
                     r=[("y2", kc), "rstd", "ptA"], w=[("hn", kc)])
            for hi in range(2):
                wa, wak = wget(l, f"ao{hi}")
                wsv, wsk = wget(l, f"so{hi}", hold=1)
                for oc in range(4):
                    o8 = hi * 4 + oc
                    ba = dense_oc(wa, wak, oc, 4, lambda kc: oatt[:, kc, 0:n], lambda kc: [("oatt", kc)], n)
                    S.op("dve", OP("tensor_tensor", out=mtmp[0][:, 0:n], in0=pb[ba][:, 0:n], in1=ga[:, o8, 0:n], op=ALU.mult),
                         r=[PB(ba), ("ga", o8)], w=["mtmp0"])
                    bs = dense_oc(wsv, wsk, oc, KC, hn_rhs, hn_keys, n)
                    S.op("dve", OP("tensor_tensor", out=mtmp[1][:, 0:n], in0=pb[bs][:, 0:n], in1=gs[:, o8, 0:n], op=ALU.mult),
                         r=[PB(bs), ("gs", o8)], w=["mtmp1"])
                    S.op("dve", OP("tensor_tensor", out=zs[:, o8, 0:n], in0=mtmp[0][:, 0:n], in1=mtmp[1][:, 0:n], op=ALU.add),
                         r=["mtmp0", "mtmp1"], w=[("zs", o8)])
            for hi in range(2):
                wv, wk = wget(l, f"wo{hi}")
                for oc in range(4):
                    o8 = hi * 4 + oc
                    b = dense_oc(wv, wk, oc, KC, lambda kc: zs[:, kc, 0:n], lambda kc: [("zs", kc)], n)
                    S.op("dve", OP("tensor_tensor", out=x[:, o8, 0:n], in0=x[:, o8, 0:n], in1=pb[b][:, 0:n], op=ALU.add),
                         r=[PB(b), ("x", o8)], w=[("x", o8)])
            rmsnorm_to(hn, lambda kc: ("hn", kc), lambda kc: ptA[:, 16 + l * 8 + kc:16 + l * 8 + kc + 1], n)
            for j in range(6):
                nocs = 4 if j < 5 else 2
                wg, wgk = wget(l, f"g{j}")
                wu, wuk = wget(l, f"u{j}", hold=1)
                for oc in range(nocs):
                    c = j * 4 + oc
                    bg = dense_oc(wg, wgk, oc, KC, hn_rhs, hn_keys, n)
                    S.op("act", OP("activation", out=ctmp[c % 2][:, 0:n], in_=pb[bg][:, 0:n], func=AF.Silu), r=[PB(bg)], w=[("ctmp", c % 2)])
                    bu = dense_oc(wu, wuk, oc, KC, hn_rhs, hn_keys, n)
                    S.op("dve", OP("tensor_tensor", out=actv(c)[:, 0:n], in0=pb[bu][:, 0:n], in1=ctmp[c % 2][:, 0:n], op=ALU.mult),
                         r=[PB(bu), ("ctmp", c % 2)], w=kact(c))
            for oc in range(8):
                wv, wk = wget(l, f"d{oc}")
                b = dense_bank()
                for kc in range(22):
                    mm(pb[b][:, 0:n], wv[:, kc, :], actv(kc)[:, 0:n], kc == 0, kc == 21, [wk] + kact(kc), [PB(b)])
                S.op("dve", OP("tensor_tensor", out=x[:, oc, 0:n], in0=x[:, oc, 0:n], in1=pb[b][:, 0:n], op=ALU.add),
                     r=[PB(b), ("x", oc)], w=[("x", oc)])

        for ti in (range(NT) if STAGE >= 2 else []):
            for s in range(NSUB):
                r0 = ti * T + s * 128
                load_tokens(x_prompt[r0:r0 + 128, :], 128, s * 128, xslot[s % 2])
            for l in range(DEPTH):
                layer(l, "prompt", ti)
            final_norm(T)
            for s in range(NSUB):
                r0 = ti * T + s * 128
                store_tokens(y_prompt[r0:r0 + 128, :], 128, s * 128, yslot[s % 2])
        if STAGE >= 3:
            load_tokens(x_sample[:, :], 32, 0, xslot[0])
            for l in range(DEPTH):
                layer(l, "sample", None)
            final_norm(32)
            store_tokens(y_sample[:, :], 32, 0, yslot[0])

        S.final_wait("pool", allslots)
        S.emit_all()
    return nc


_CACHE = {}


def kernel(**inputs):
    consts = static_consts()
    bucket = consts.pop("c_bucket")
    seq = inputs["x_prompt"].shape[1]
    key = (seq,)
    if key not in _CACHE:
        _CACHE[key] = build(seq, 2, bucket)
    nc = _CACHE[key]
    f = lambda a: np.ascontiguousarray(np.asarray(a, dtype=np.float32))
    shared = {k: f(inputs[k]) for k in ("rel_table", "g_mix", "w_in", "conv_w", "conv_b", "dt_bias", "a_log", "d_skip",
                                         "g_ssd", "sinks", "w_att_out", "w_ssd_out", "w_out", "g_ffn", "w_gate", "w_up",
                                         "w_down", "g_final")}
    shared.update(consts)
    xp = f(inputs["x_prompt"])
    nb = xp.shape[0]
    xs = f(inputs["x_sample"])
    ck, cv = f(inputs["cache_k"]), f(inputs["cache_v"])
    sc, sm = f(inputs["state_conv"]), f(inputs["state_ssm"])
    in_maps = []
    for c in range(NCORES):
        m = dict(shared)
        m["x_prompt"] = xp[c % nb]
        sl = slice(2 * c, 2 * c + 2)
        m["x_sample"] = np.ascontiguousarray(xs[sl].reshape(32, D))
        m["cache_k"] = np.ascontiguousarray(ck[:, sl].reshape(DEPTH, 2, 128, 128))
        m["cache_v"] = np.ascontiguousarray(cv[:, sl].reshape(DEPTH, 2, 128, 128))
        m["state_conv"] = np.ascontiguousarray(sc[:, sl].reshape(DEPTH, 6, 1536))
        m["state_ssm"] = np.ascontiguousarray(sm[:, sl].reshape(DEPTH, 2, 1024, 128))
        in_maps.append(m)
    if os.environ.get("KTRACE"):
        res = run_bass_kernel_spmd(nc, in_maps, core_ids=list(range(NCORES)), trace=True)
        print("EXEC_NS", res.exec_time_ns)
    else:
        res = run_bass_kernel_spmd(nc, in_maps, core_ids=list(range(NCORES)))
    R = res.results
    y_prompt = np.stack([R[b]["y_prompt"] for b in range(nb)])
    k_prompt = np.stack([R[b]["k_prompt"] for b in range(nb)], axis=1).reshape(DEPTH, nb, 128, 2, 64)
    v_prompt = np.stack([R[b]["v_prompt"] for b in range(nb)], axis=1).reshape(DEPTH, nb, 128, 2, 64)
    conv_prompt = np.stack([R[b]["conv_prompt"] for b in range(nb)], axis=1)
    ssm_prompt = np.stack([R[b]["ssm_prompt"] for b in range(nb)], axis=1).reshape(DEPTH, nb, 16, 64, 128)
    y_sample = np.concatenate([R[c]["y_sample"].reshape(2, 16, D) for c in range(NCORES)], axis=0)
    k_sample = np.concatenate([R[c]["k_sample"].reshape(DEPTH, 2, 128, 2, 64) for c in range(NCORES)], axis=1)
    v_sample = np.concatenate([R[c]["v_sample"].reshape(DEPTH, 2, 128, 2, 64) for c in range(NCORES)], axis=1)
    conv_sample = np.concatenate([R[c]["conv_sample"] for c in range(NCORES)], axis=1)
    ssm_sample = np.concatenate([R[c]["ssm_sample"].reshape(DEPTH, 2, 16, 64, 128) for c in range(NCORES)], axis=1)
    return (y_prompt, y_sample, k_prompt, v_prompt, conv_prompt, ssm_prompt, k_sample, v_sample, conv_sample, ssm_sample)
```

```python
import os
import numpy as np
from contextlib import ExitStack
import concourse.bass as bass
import concourse.mybir as mybir
from concourse.bass_utils import run_bass_kernel_spmd

F32 = mybir.dt.float32
BF16 = mybir.dt.bfloat16
AF = mybir.ActivationFunctionType
ALU = mybir.AluOpType

D = 1024
KC = 8
T = 512
NSUB = 4
DEPTH = 2
INW = 5392
DFF = 2816
EPS = 1e-6
NCORES = 8
SEQ_FULL = 4096
NSLOT = 3
SLOTW = 4096


def OP(name, *args, **kwargs):
    def f(e):
        return getattr(e, name)(*args, **kwargs)
    return f


class Slot:
    def __init__(self, sem):
        self.sem = sem
        self.val = 0


class Sched:
    ENG = ("pe", "dve", "act", "pool", "sp")

    def __init__(self, nc, sems):
        self.nc = nc
        self.sem = dict(zip(self.ENG, sems))
        self.cnt = {e: 0 for e in self.ENG}
        self.ops = {e: [] for e in self.ENG}
        self.seen = {e: {} for e in self.ENG}
        self.last_w = {}
        self.readers = {}
        self.semobj = {}

    def _tid(self, sem):
        i = id(sem)
        self.semobj[i] = sem
        return i

    def _deps(self, eng, r, w):
        need = {}

        def add(tok):
            if tok is None:
                return
            sid, val, src = tok
            if src == "pe" and eng == "pe":
                return
            if need.get(sid, 0) < val:
                need[sid] = val

        for k in r:
            add(self.last_w.get(k))
        for k in w:
            add(self.last_w.get(k))
            for tok in self.readers.get(k, ()):
                add(tok)
        waits = []
        seen = self.seen[eng]
        for sid, val in need.items():
            if seen.get(sid, 0) < val:
                seen[sid] = val
                waits.append((self.semobj[sid], val))
        return waits

    def _commit(self, tok, r, w):
        for k in r:
            lst = self.readers.setdefault(k, [])
            lst[:] = [t for t in lst if t[0] != tok[0]]
            lst.append(tok)
        for k in w:
            self.last_w[k] = tok
            self.readers[k] = []

    def op(self, eng, fn, r=(), w=()):
        self.nrec = getattr(self, "nrec", 0) + 1
        if self.nrec > int(os.environ.get("KCUT", "100000000")):
            return
        waits = self._deps(eng, r, w)
        self.cnt[eng] += 1
        sem = self.sem[eng]
        tok = (self._tid(sem), self.cnt[eng], eng)
        self._commit(tok, r, w)

        def emit(e, waits=waits, fn=fn, sem=sem):
            for s, v in waits:
                e.wait_ge(s, v)
            fn(e).then_inc(sem, 1)

        self.ops[eng].append(emit)

    def dma(self, eng, slot, out, in_, r=(), w=(), serial=False, batch=False, **kw):
        self.nrec = getattr(self, "nrec", 0) + 1
        if self.nrec > int(os.environ.get("KCUT", "100000000")):
            return
        waits = self._deps(eng, r, w)
        if serial and slot.val:
            sid = self._tid(slot.sem)
            if self.seen[eng].get(sid, 0) < slot.val:
                self.seen[eng][sid] = slot.val
                waits.append((slot.sem, slot.val))
        slot.val += 16
        tok = (self._tid(slot.sem), slot.val, "dma")
        self._commit(tok, r, w)
        if batch:
            slot.bkeys = getattr(slot, "bkeys", set()) | set(w)

        def emit(e, waits=waits, slot=slot, out=out, in_=in_, kw=kw):
            for s, v in waits:
                e.wait_ge(s, v)
            e.dma_start(out=out, in_=in_, **kw).then_inc(slot.sem, 16)

        self.ops[eng].append(emit)

    def close_batch(self, slot):
        tok = (self._tid(slot.sem), slot.val, "dma")
        for k in getattr(slot, "bkeys", ()):
            self.last_w[k] = tok
        slot.bkeys = set()

    def final_wait(self, eng, slots):
        def emit(e, slots=slots):
            for s in slots:
                if s.val:
                    e.wait_ge(s.sem, s.val)

        self.ops[eng].append(emit)

    def emit_all(self):
        with self.nc.Block() as block:

            @block.tensor
            def _(e):
                for f in self.ops["pe"]:
                    f(e)

            @block.vector
            def _(e):
                for f in self.ops["dve"]:
                    f(e)

            @block.scalar
            def _(e):
                for f in self.ops["act"]:
                    f(e)

            @block.gpsimd
            def _(e):
                for f in self.ops["pool"]:
                    f(e)

            @block.sync
            def _(e):
                for f in self.ops["sp"]:
                    f(e)


def layer_units():
    u = [("q", "w_in", 1024, 0, 512), ("kv", "w_in", 1024, 512, 512),
         ("z0", "w_in", 1024, 768, 512), ("z1", "w_in", 1024, 1280, 512),
         ("xs0", "w_in", 1024, 1792, 512), ("xs1", "w_in", 1024, 2304, 512),
         ("bc", "w_in", 1024, 2816, 512), ("dt", "w_in", 1024, 3328, 16),
         ("ga0", "w_in", 1024, 3344, 512), ("ga1", "w_in", 1024, 3856, 512),
         ("gs0", "w_in", 1024, 4368, 512), ("gs1", "w_in", 1024, 4880, 512),
         ("ao0", "w_att_out", 512, 0, 512), ("so0", "w_ssd_out", 1024, 0, 512),
         ("ao1", "w_att_out", 512, 512, 512), ("so1", "w_ssd_out", 1024, 512, 512),
         ("wo0", "w_out", 1024, 0, 512), ("wo1", "w_out", 1024, 512, 512)]
    for j in range(6):
        n = 512 if j < 5 else 256
        u.append((f"g{j}", "w_gate", 1024, j * 512, n))
        u.append((f"u{j}", "w_up", 1024, j * 512, n))
    for oc in range(8):
        u.append((f"d{oc}", "w_down", 2816, oc * 128, 128))
    return u


def static_consts():
    c = {}
    i = np.arange(128)
    c["c_ident"] = np.eye(128, dtype=np.float32)
    c["c_J"] = np.eye(128, dtype=np.float32)[::-1].copy()
    c["c_Uinc"] = (i[:, None] <= i[None, :]).astype(np.float32)
    c["c_Ustr"] = (i[:, None] > i[None, :]).astype(np.float32)
    c["c_maskU"] = (i[:, None] <= i[None, :]).astype(np.float32)
    rel = 127 - np.arange(383)
    nb, me = 16, 8
    n = np.abs(rel)
    nf = np.maximum(n, 1).astype(np.float32)
    large = me + (np.log(nf / me) / np.log(128 / me) * (nb - me)).astype(np.int32)
    large = np.minimum(large, nb - 1)
    bucket = np.where(rel > 0, nb, 0) + np.where(n < me, n, large)
    c["c_bucket"] = bucket.astype(np.int64)
    oh = np.zeros((32, 384), dtype=np.float32)
    oh[bucket, np.arange(383)] = 1.0
    c["c_OH"] = oh
    return c


def build(seq, nsamp, bucket):
    NT = seq // T
    nc = bass.Bass("TRN2", target_bir_lowering=False)
    dt_ = lambda name, shape, dtype=F32, kind="ExternalInput": nc.dram_tensor(name, list(shape), dtype, kind=kind).ap()
    x_prompt = dt_("x_prompt", [seq, D])
    rel_table = dt_("rel_table", [32, 8])
    g_mix = dt_("g_mix", [DEPTH, D])
    w_in = dt_("w_in", [DEPTH, D, INW])
    conv_w = dt_("conv_w", [DEPTH, 4, 1536])
    conv_b = dt_("conv_b", [DEPTH, 1536])
    dt_bias = dt_("dt_bias", [DEPTH, 16])
    a_log = dt_("a_log", [DEPTH, 16])
    d_skip = dt_("d_skip", [DEPTH, 16])
    g_ssd = dt_("g_ssd", [DEPTH, D])
    sinks = dt_("sinks", [DEPTH, 8])
    W = {"w_in": w_in,
         "w_att_out": dt_("w_att_out", [DEPTH, 512, D]),
         "w_ssd_out": dt_("w_ssd_out", [DEPTH, D, D]),
         "w_out": dt_("w_out", [DEPTH, D, D]),
         "w_gate": dt_("w_gate", [DEPTH, D, DFF]),
         "w_up": dt_("w_up", [DEPTH, D, DFF]),
         "w_down": dt_("w_down", [DEPTH, DFF, D])}
    g_ffn = dt_("g_ffn", [DEPTH, D])
    g_final = dt_("g_final", [D])
    c_ident = dt_("c_ident", [128, 128])
    c_J = dt_("c_J", [128, 128])
    c_Uinc = dt_("c_Uinc", [128, 128])
    c_Ustr = dt_("c_Ustr", [128, 128])
    c_maskU = dt_("c_maskU", [128, 128])
    c_OH = dt_("c_OH", [32, 384])

    x_sample = dt_("x_sample", [32, D])
    cache_k = dt_("cache_k", [DEPTH, 2, 128, 128])
    cache_v = dt_("cache_v", [DEPTH, 2, 128, 128])
    state_conv = dt_("state_conv", [DEPTH, 6, 1536])
    state_ssm = dt_("state_ssm", [DEPTH, 2, 1024, 128])
    y_sample = dt_("y_sample", [32, D], kind="ExternalOutput")
    k_samp = dt_("k_sample", [DEPTH, 2, 128, 128], kind="ExternalOutput")
    v_samp = dt_("v_sample", [DEPTH, 2, 128, 128], kind="ExternalOutput")
    conv_samp = dt_("conv_sample", [DEPTH, 2, 3, 1536], kind="ExternalOutput")
    ssm_samp = dt_("ssm_sample", [DEPTH, 2, 1024, 128], kind="ExternalOutput")
    y_prompt = dt_("y_prompt", [seq, D], kind="ExternalOutput")
    k_out = dt_("k_prompt", [DEPTH, 128, 128], kind="ExternalOutput")
    v_out = dt_("v_prompt", [DEPTH, 128, 128], kind="ExternalOutput")
    conv_out = dt_("conv_prompt", [DEPTH, 3, 1536], kind="ExternalOutput")
    ssm_out = dt_("ssm_prompt", [DEPTH, 16 * 64, 128], kind="ExternalOutput")

    units = layer_units()
    uoff = {}
    off = 0
    for (name, wn, K, c0, n) in units:
        uoff[name] = off
        off += (K // 128) * n
    WSC_COLS = off
    wsc = [dt_(f"wsc{l}", [128, WSC_COLS], BF16, kind="Internal") for l in range(DEPTH)]
    bvd = dt_("bvd", [8, 384], F32, kind="Internal")

    es = ExitStack()
    with es:
        sems = [es.enter_context(nc.semaphore(f"e{i}")) for i in range(5)]
        S = Sched(nc, sems)
        nslot_ctr = [0]

        def newslot():
            nslot_ctr[0] += 1
            return Slot(es.enter_context(nc.semaphore(f"d{nslot_ctr[0]}")))

        def sb(name, shape, dtype=F32):
            return es.enter_context(nc.sbuf_tensor(name, list(shape), dtype))

        allslots = []

        def mkslot():
            s = newslot()
            allslots.append(s)
            return s

        x = sb("x", [128, KC, T])
        hn = sb("hn", [128, KC, T], BF16)
        zs = sb("zs", [128, KC, T], BF16)
        qT = sb("qT", [64, 8, T], BF16)
        oatt = sb("oatt", [128, 4, T], BF16)
        kd = [[sb(f"kd{l}{g}", [128, 128 + T], BF16) for g in range(2)] for l in range(DEPTH)]
        vt = [sb(f"vt{l}", [128, NSUB + 1, 128], BF16) for l in range(DEPTH)]
        big = sb("big", [128, 12800], BF16)
        hist = [sb(f"hist{l}", [128, 12, 3], BF16) for l in range(DEPTH)]
        ga = sb("ga", [128, KC, T], BF16)
        gs = sb("gs", [128, KC, T], BF16)
        y2 = sb("y2", [128, KC, T], BF16)
        wr = [sb(f"wr{i}", [128, SLOTW], BF16) for i in range(NSLOT)]
        stT = [sb(f"stT{l}", [128, 1024]) for l in range(DEPTH)]
        stB = [sb(f"stB{l}", [128, 1024], BF16) for l in range(DEPTH)]
        scr = sb("scr", [128, 4096])
        rhsh = scr[:, 0:1024].rearrange("p (h l) -> p h l", h=8)
        Esb = scr[:, 1024:2048].rearrange("p (h l) -> p h l", h=8)
        LT = scr[:, 2048:3072].rearrange("p (h l) -> p h l", h=8)
        sq = scr[:, 0:2048].bitcast(BF16).rearrange("p (k t) -> p k t", k=KC)
        mtmp = [scr[:, 3072:3584], scr[:, 3584:4096]]
        bvT = scr[0:8, 2048:2432]
        convT = scr[0:3, 2048:3584].rearrange("p (c q) -> p c q", c=12)
        Wt = sb("Wt", [128, 16, 128], BF16)
        CE = sb("CE", [128, 16, 128], BF16)
        xdt = sb("xdt", [128, 1024], BF16)
        xsc = sb("xsc", [128, 1024], BF16)
        Btok = sb("Btok", [128, 2, 128], BF16)
        CBm = sb("CBm", [128, 2, 128], BF16)
        pexp = sb("pexp", [128, 512])
        pT = [sb(f"pT{i}", [128, 512], BF16) for i in range(2)]
        rec = pexp
        expB = sb("expB", [128, 8, 2, 128], BF16)
        stg = sb("stg", [128, D])
        ctmp = [stg[:, 0:T], stg[:, T:2 * T]]
        rstd = sb("rstd", [128, T])
        stag = [stg, stg]
        SK = [("ctmp", 0), ("ctmp", 1)]
        ident = sb("ident", [128, 128])
        identb = sb("identb", [128, 128], BF16)
        Jm = sb("Jm", [128, 128])
        Uinc = sb("Uinc", [128, 128])
        Ustr = sb("Ustr", [128, 128])
        maskU = sb("maskU", [128, 128])
        onesb = sb("onesb", [128, 128], BF16)
        onesf = sb("onesf", [128, 128])
        hank = sb("hank", [128, 8, 2, 128])
        pstA = sb("pstA", [80, 128])
        pstB = sb("pstB", [96, 128])
        ptA = sb("ptA", [128, 80])
        ptB = sb("ptB", [128, 96])
        OHs = sb("OHs", [32, 384])
        tbs = sb("tbs", [32, 8])
        dfull = sb("dfull", [128, DEPTH, 16])
        sfull = sb("sfull", [128, DEPTH, 8])
        dtb = sb("dtb", [128, DEPTH, 16])
        abc = sb("abc", [128, DEPTH, 16])
        dsk = sb("dsk", [128, DEPTH, 8])
        diagD = sb("diagD", [128, DEPTH, 8, 128], BF16)
        esink = sb("esink", [128, DEPTH, 4])
        dtt = sb("dtt", [128, 16])
        dat = sb("dat", [128, 16])
        dtraw = sb("dtraw", [128, NSUB, 16])
        rh2 = sb("rh2", [128, 1024])
        decend = sb("decend", [128, 16])
        kvo = sb("kvo", [128, 256])
        convo = sb("convo", [128, 2, 12, 3])
        sso = scr[:, 2048:3072].rearrange("p (c n) -> p c n", c=8)
        print("SBUF remaining", nc.sbuf_bytes_remaining)
        pb = [es.enter_context(nc.psum_tensor(f"pb{i}", [128, 512], F32)) for i in range(8)]

        PB = lambda i: ("pb", i)

        def bigpages(lo, hi):
            return [("big", p) for p in range(lo // 512, (hi - 1) // 512 + 1)]

        def xbcp(c):
            return big[:, 515 * c:515 * c + 515]

        XC0 = 6400

        def xc(c):
            return big[:, XC0 + 512 * c:XC0 + 512 * c + 512]

        def actv(c):
            return big[:, 512 * c:512 * c + 512]

        kxbcp = lambda c: bigpages(515 * c, 515 * c + 515)
        kxc = lambda c: bigpages(XC0 + 512 * c, XC0 + 512 * c + 512)
        kact = lambda c: bigpages(512 * c, 512 * c + 512)

        ldslot = mkslot()
        def ld(dst, src, key, q="sp", **kw):
            S.dma(q, ldslot, dst, src, w=[key], batch=True, **kw)

        ld(ident[:], c_ident[:, :], "ident")
        ld(Jm[:], c_J[:, :], "Jm")
        ld(Uinc[:], c_Uinc[:, :], "Uinc")
        ld(Ustr[:], c_Ustr[:, :], "Ustr")
        ld(maskU[:], c_maskU[:, :], "maskU")
        NC_ = dict(allow_slow_non_contiguous=True)
        ld(pstA[0:16, :], g_mix.rearrange("l (k p) -> (l k) p", p=128), "pstA")
        ld(pstA[16:32, :], g_ffn.rearrange("l (k p) -> (l k) p", p=128), "pstA")
        ld(pstA[32:48, :], g_ssd.rearrange("l (k p) -> (l k) p", p=128), "pstA")
        ld(pstA[48:56, :], g_final.rearrange("(k p) -> k p", p=128), "pstA")
        ld(pstA[56:80, :], conv_b.rearrange("l (c p) -> (l c) p", p=128), "pstA")
        ld(pstB[:, :], conv_w.rearrange("l j (c p) -> (l j c) p", p=128), "pstB")
        ld(dtb[:], dt_bias.partition_broadcast(128), "dtb")
        ld(abc[:], a_log.partition_broadcast(128), "abc")
        ld(dfull[:], d_skip.partition_broadcast(128), "dfull")
        ld(sfull[:], sinks.partition_broadcast(128), "sfull")
        ld(OHs[:], c_OH[:, :], "OHs")
        ld(tbs[:], rel_table[:, :], "tbs")
        S.close_batch(ldslot)
        S.op("pe", OP("transpose", pb[0][:, 0:80], pstA[:, :], ident[0:80, 0:80]), r=["pstA", "ident"], w=[PB(0)])
        S.op("pe", OP("transpose", pb[0][:, 128:224], pstB[:, :], ident[0:96, 0:96]), r=["pstB", "ident"], w=[PB(0)])
        S.op("dve", OP("tensor_copy", out=ptA[:], in_=pb[0][:, 0:80]), r=[PB(0)], w=["ptA"])
        S.op("dve", OP("tensor_copy", out=ptB[:], in_=pb[0][:, 128:224]), r=[PB(0)], w=["ptB"])
        S.op("dve", OP("tensor_scalar", out=ptA[:, 0:56], in0=ptA[:, 0:56], scalar1=32.0, scalar2=None, op0=ALU.mult),
             r=["ptA"], w=["ptA"])
        for hh in range(2):
            pr = slice(hh * 64, hh * 64 + 64)
            S.op("dve", OP("tensor_copy", out=dsk[pr], in_=dfull[pr].rearrange("p l (j t) -> p l j t", t=2)[:, :, :, hh]),
                 r=["dfull"], w=["dsk"])
            S.op("dve", OP("tensor_copy", out=esink[pr], in_=sfull[pr].rearrange("p l (j t) -> p l j t", t=2)[:, :, :, hh]),
                 r=["sfull"], w=["esink"])
        S.op("dve", OP("memset", onesb[:], 1.0), w=["onesb"])
        S.op("dve", OP("memset", onesf[:], 1.0), w=["onesf"])
        S.op("dve", OP("tensor_copy", out=identb[:], in_=ident[:]), r=["ident"], w=["identb"])
        S.op("act", OP("activation", out=abc[:], in_=abc[:], func=AF.Exp), r=["abc"], w=["abc"])
        S.op("dve", OP("tensor_scalar", out=abc[:], in0=abc[:], scalar1=-1.0, scalar2=None, op0=ALU.mult),
             r=["abc"], w=["abc"])
        S.op("act", OP("activation", out=esink[:], in_=esink[:], func=AF.Exp), r=["esink"], w=["esink"])
        for l in range(DEPTH):
            for j in range(8):
                S.op("dve", OP("tensor_scalar", out=diagD[:, l, j, :], in0=ident[:], scalar1=dsk[:, l, j:j + 1],
                                                                scalar2=None, op0=ALU.mult),
                     r=["ident", "dsk"], w=["diagD"])
        for l in range(DEPTH):
            S.op("dve", OP("memset", stT[l][:], 0.0), w=[("stT", l)])
            S.op("dve", OP("memset", stB[l][:], 0.0), w=[("stB", l)])
            S.op("dve", OP("memset", hist[l][:], 0.0), w=[("hist", l)])

        bslot = mkslot()
        hslot = mkslot()
        S.op("pe", OP("matmul", pb[1][0:8, 0:384], lhsT=tbs[:, :], rhs=OHs[:, :], start=True, stop=True), r=["tbs", "OHs"], w=[PB(1)])
        S.op("dve", OP("tensor_copy", out=bvT, in_=pb[1][0:8, 0:384]), r=[PB(1)], w=[("scr", 2)])
        S.dma("sp", bslot, bvd[:, :], bvT, r=[("scr", 2)], w=["bvd"])
        for h in range(8):
            for blk in range(2):
                C = 128 * (1 - blk)
                src = bass.AP(bvd.tensor, h * 384 + C, [[1, 128], [1, 128]])
                S.dma("act", hslot, hank[:, h, blk, :], src, r=["bvd"], w=[("hank", h, blk)], batch=True)
        S.close_batch(hslot)
        for h in range(8):
            S.op("pe", OP("matmul", pb[h % 2][:, 0:256], lhsT=Jm[:], rhs=hank[:, h].rearrange("p b q -> p (b q)"),
                                               start=True, stop=True),
                 r=["Jm", ("hank", h, 0), ("hank", h, 1)], w=[PB(h % 2)])
            S.op("act", OP("activation", out=expB[:, h].rearrange("p b q -> p (b q)"), in_=pb[h % 2][:, 0:256],
                                                    func=AF.Exp),
                 r=[PB(h % 2)], w=["expB"])
        S.op("dve", OP("tensor_scalar", out=expB[0:64, :, 0, 64:128], in0=expB[0:64, :, 0, 64:128], scalar1=0.0, scalar2=None, op0=ALU.mult), r=["expB"], w=["expB"])
        S.op("dve", OP("tensor_scalar", out=expB[64:128, :, 1, 0:64], in0=expB[64:128, :, 1, 0:64], scalar1=0.0, scalar2=None, op0=ALU.mult), r=["expB"], w=["expB"])

        STAGE = int(os.environ.get('KSTAGE', '99'))
        cslots = [mkslot(), mkslot()]
        cctr = [0]

        def cslot_next():
            cctr[0] += 1
            return cslots[cctr[0] % 2]
        for l in (range(DEPTH) if STAGE >= 1 else []):
            for (name, wn, K, c0, n) in units:
                kc = K // 128
                o = uoff[name]
                dst = wsc[l][:, o:o + kc * n].rearrange("p (k c) -> p k c", k=kc)
                wl = W[wn][l]
                if name == "kv":
                    for g in range(2):
                        for rep in range(2):
                            S.dma("pool", cslot_next(), dst[:, :, (2 * g + rep) * 64:(2 * g + rep) * 64 + 64],
                                  wl[:, 512 + g * 64:512 + g * 64 + 64].rearrange("(k p) c -> p k c", p=128),
                                  w=[("wsc", l, name)], serial=True)
                    S.dma("pool", cslot_next(), dst[:, :, 256:512], wl[:, 512:768].rearrange("(k p) c -> p k c", p=128),
                          w=[("wsc", l, name)], serial=True)
                else:
                    S.dma("pool", cslot_next(), dst, wl[:, c0:c0 + n].rearrange("(k p) c -> p k c", p=128), w=[("wsc", l, name)], serial=True)

        wslots = [mkslot() for _ in range(NSLOT)]
        order = [(l, u) for _t in range(NT + 1) for l in range(DEPTH) for u in units]
        ws = {"issued": 0, "pos": 0}

        def wget(l, name, hold=0):
            i = ws["pos"]
            assert order[i][0] == l and order[i][1][0] == name, (order[i], l, name)
            while ws["issued"] < min(len(order), i + NSLOT - hold):
                j = ws["issued"]
                ll, (nm, wn, K, c0, n) = order[j]
                kc = K // 128
                o = uoff[nm]
                S.dma("sp", wslots[j % NSLOT], wr[j % NSLOT][:, 0:kc * n], wsc[ll][:, o:o + kc * n],
                      r=[("wsc", ll, nm)], w=[("wr", j % NSLOT)])
                ws["issued"] += 1
            ws["pos"] += 1
            _, (nm, wn, K, c0, n) = order[i]
            kc = K // 128
            return wr[i % NSLOT][:, 0:kc * n].rearrange("p (k c) -> p k c", k=kc), ("wr", i % NSLOT)

        rr = {"ev": 0, "pbd": 0}

        def dense_bank():
            rr["pbd"] ^= 1
            return rr["pbd"]

        def mm(out, lhsT, rhs, start, stop, r, w):
            S.op("pe", OP("matmul", out, lhsT=lhsT, rhs=rhs, start=start, stop=stop), r=r, w=w)

        xslot = [mkslot(), mkslot()]
        yslot = [mkslot(), mkslot()]
        kvslot = mkslot()
        cvslot = mkslot()
        ssslot = mkslot()
        kcslot = mkslot()

        def rsq(src, dst, skey):
            S.op("dve", OP("tensor_scalar", out=dst, in0=src, scalar1=1024.0 * EPS, scalar2=None, op0=ALU.add),
                 r=[skey], w=["rstd"])
            S.op("act", OP("activation", out=dst, in_=dst, func=AF.Ln), r=["rstd"], w=["rstd"])
            S.op("act", OP("activation", out=dst, in_=dst, func=AF.Exp, scale=-0.5), r=["rstd"], w=["rstd"])

        def rmsnorm_to(dst, dkeyf, gcol, n=T):
            for kc in range(KC):
                S.op("act", OP("activation", out=sq[:, kc, 0:n], in_=x[:, kc, 0:n], func=AF.Square),
                     r=[("x", kc)], w=[("scr", 0), ("scr", 1)])
            b = 2
            for kc in range(KC):
                mm(pb[b][:, 0:n], onesb[:], sq[:, kc, 0:n], kc == 0, kc == KC - 1, ["onesb", ("scr", 0), ("scr", 1)], [PB(b)])
            rsq(pb[b][:, 0:n], rstd[:, 0:n], PB(b))
            for kc in range(KC):
                eng = "dve"
                S.op(eng, OP("scalar_tensor_tensor", out=dst[:, kc, 0:n], in0=x[:, kc, 0:n], scalar=gcol(kc),
                                                                  in1=rstd[:, 0:n], op0=ALU.mult, op1=ALU.mult),
                     r=[("x", kc), "rstd", "ptA"], w=[dkeyf(kc)])

        def dense_oc(wv, wkey, ocl, nk, rhs_of, rkeys, n=T):
            b = dense_bank()
            for kc in range(nk):
                mm(pb[b][:, 0:n], wv[:, kc, ocl * 128:(ocl + 1) * 128], rhs_of(kc), kc == 0, kc == nk - 1,
                   [wkey] + rkeys(kc), [PB(b)])
            return b

        def evac_copy(dst, b, wkeys, n=T):
            rr["ev"] ^= 1
            if rr["ev"]:
                S.op("act", OP("activation", out=dst, in_=pb[b][:, 0:n], func=AF.Copy), r=[PB(b)], w=wkeys)
            else:
                S.op("dve", OP("tensor_copy", out=dst, in_=pb[b][:, 0:n]), r=[PB(b)], w=wkeys)

        def evac_act(dst, b, func, wkeys, n=T):
            S.op("act", OP("activation", out=dst, in_=pb[b][:, 0:n], func=func), r=[PB(b)], w=wkeys)

        hn_rhs = lambda kc: hn[:, kc, :]
        hn_keys = lambda kc: [("hn", kc)]

        def load_tokens(src_rows_ap, rows, col0, slot):
            S.dma("sp", slot, stg[0:rows, :], src_rows_ap, w=SK)
            for half in range(2):
                b = 4 + half
                for c4 in range(4):
                    kc = half * 4 + c4
                    S.op("pe", OP("transpose", pb[b][:, c4 * rows:(c4 + 1) * rows], stg[0:rows, kc * 128:(kc + 1) * 128],
                                  ident[0:rows, 0:rows]), r=SK + ["ident"], w=[PB(b)])
                dstv = x[:, half * 4:half * 4 + 4, col0:col0 + rows]
                srcv = pb[b][:, 0:4 * rows].rearrange("p (c t) -> p c t", c=4)
                wk_ = [("x", half * 4 + c) for c in range(4)]
                if half == 0:
                    S.op("dve", OP("tensor_copy", out=dstv, in_=srcv), r=[PB(b)], w=wk_)
                else:
                    S.op("act", OP("activation", out=dstv, in_=srcv, func=AF.Copy), r=[PB(b)], w=wk_)

        def store_tokens(dst_rows_ap, rows, col0, slot):
            for half in range(2):
                b = 4 + half
                for c4 in range(4):
                    kc = half * 4 + c4
                    S.op("pe", OP("transpose", pb[b][0:rows, c4 * 128:(c4 + 1) * 128], x[:, kc, col0:col0 + rows], ident[:]),
                         r=[("x", kc), "ident"], w=[PB(b)])
                if half == 0:
                    S.op("dve", OP("tensor_copy", out=stg[0:rows, 0:512], in_=pb[b][0:rows, :]), r=[PB(b)], w=SK)
                else:
                    S.op("act", OP("activation", out=stg[0:rows, 512:1024], in_=pb[b][0:rows, :], func=AF.Copy), r=[PB(b)], w=SK)
            S.dma("pool", slot, dst_rows_ap, stg[0:rows, :], r=SK)

        def final_norm(n):
            for kc in range(KC):
                S.op("act", OP("activation", out=sq[:, kc, 0:n], in_=x[:, kc, 0:n], func=AF.Square),
                     r=[("x", kc)], w=[("scr", 0), ("scr", 1)])
            for kc in range(KC):
                mm(pb[2][:, 0:n], onesb[:], sq[:, kc, 0:n], kc == 0, kc == KC - 1, ["onesb", ("scr", 0), ("scr", 1)], [PB(2)])
            rsq(pb[2][:, 0:n], rstd[:, 0:n], PB(2))
            for kc in range(KC):
                S.op("dve", OP("scalar_tensor_tensor", out=x[:, kc, 0:n], in0=x[:, kc, 0:n], scalar=ptA[:, 48 + kc:48 + kc + 1],
                               in1=rstd[:, 0:n], op0=ALU.mult, op1=ALU.mult),
                     r=[("x", kc), "rstd", "ptA"], w=[("x", kc)])

        def v3(base, parts, R):
            return scr[0:parts, base:base + 8 * R].rearrange("p (h l) -> p h l", h=8)

        def ssd_block(l, R, cs, dti):
            rows = slice(0, R)
            rh, Ev, LTv = v3(0, R, R), v3(1024, 128, R), v3(2048, R, R)
            HPM = min(8, 512 // R)
            S.op("act", OP("activation", out=dtt[rows, :], in_=dtraw[rows, dti, :], func=AF.Exp), r=["dtraw"], w=["dtt"])
            S.op("act", OP("activation", out=dtt[rows, :], in_=dtt[rows, :], func=AF.Ln, bias=1.0), r=["dtt"], w=["dtt"])
            S.op("dve", OP("tensor_tensor", out=dat[rows, :], in0=dtt[rows, :], in1=abc[rows, l, :], op=ALU.mult),
                 r=["dtt", "abc"], w=["dat"])
            scr2b = scr[:, 2048:3072].bitcast(BF16)
            scr1b = scr[:, 1024:2048].bitcast(BF16)
            rhs_ = [rh, rh2[0:R, 0:8 * R].rearrange("p (h l) -> p h l", h=8)]
            rhk = [("scr", 0), "rh2"]
            for g in range(2):
                hs = slice(g * 8, g * 8 + 8)
                S.op("dve", OP("tensor_tensor", out=rhs_[g], in0=Uinc[rows, 0:R].unsqueeze(1).to_broadcast([R, 8, R]),
                               in1=dat[rows, hs].unsqueeze(2).to_broadcast([R, 8, R]), op=ALU.mult),
                     r=["Uinc", "dat"], w=[rhk[g]])
            pxt = pb[4][:, :].bitcast(BF16)
            for j in range(8):
                S.op("pe", OP("transpose", pxt[rows, j * 128:(j + 1) * 128], xc(j)[:, cs], identb[:]),
                     r=kxc(j) + ["identb"], w=[PB(4)])
            pbt = pb[5][:, :].bitcast(BF16)
            for g in range(2):
                S.op("pe", OP("transpose", pbt[rows, g * 128:(g + 1) * 128], xc(8 + g)[:, cs], identb[:]),
                     r=kxc(8 + g) + ["identb"], w=[PB(5)])
            S.op("act", OP("activation", out=Btok[rows].rearrange("p g n -> p (g n)"), in_=pbt[rows, 0:256], func=AF.Copy),
                 r=[PB(5)], w=["Btok"])
            S.op("dve", OP("tensor_tensor", out=xdt[rows].rearrange("p (h d) -> p h d", h=16),
                           in0=pxt[rows, :].rearrange("p (h d) -> p h d", h=16),
                           in1=dtt[rows, :].unsqueeze(2).to_broadcast([R, 16, 64]), op=ALU.mult),
                 r=[PB(4), "dtt"], w=["xdt"])
            for g in range(2):
                mm(pb[5][rows, 256 + g * R:256 + (g + 1) * R], xc(8 + g)[:, cs], xc(10 + g)[:, cs], True, True,
                   kxc(8 + g) + kxc(10 + g), [PB(5)])
            S.op("dve", OP("tensor_tensor", out=CBm[rows, :, 0:R], in0=pb[5][rows, 256:256 + 2 * R].rearrange("p (g l) -> p g l", g=2),
                           in1=maskU[rows, 0:R].unsqueeze(1).to_broadcast([R, 2, R]), op=ALU.mult),
                 r=[PB(5), "maskU"], w=["CBm"])
            LTg = [scr2b[0:R, g * 1024:g * 1024 + 8 * R].rearrange("p (h l) -> p h l", h=8) for g in range(2)]
            Eg = [scr1b[:, g * 1024:g * 1024 + 8 * R].rearrange("p (h l) -> p h l", h=8) for g in range(2)]
            bank = [[2, 3], [6, 7]]
            for g in range(2):
                for half in range(8 // HPM):
                    hsl = slice(half * HPM, (half + 1) * HPM)
                    bk = bank[g][half]
                    mm(pb[bk][rows, 0:HPM * R], Ustr[rows, 0:R], rhs_[g][:, hsl, :].rearrange("p h l -> p (h l)"), True, True,
                       ["Ustr", rhk[g]], [PB(bk)])
                    S.op("act", OP("activation", out=LTg[g][:, hsl, :].rearrange("p h l -> p (h l)"), in_=pb[bk][rows, 0:HPM * R],
                                   func=AF.Exp), r=[PB(bk)], w=[("scr", 2)])
            for g in range(2):
                for half in range(8 // HPM):
                    hsl = slice(half * HPM, (half + 1) * HPM)
                    bk = bank[g][half]
                    mm(pb[bk][:, 0:HPM * R], onesf[rows, :], rhs_[g][:, hsl, :].rearrange("p h l -> p (h l)"), True, True,
                       ["onesf", rhk[g]], [PB(bk)])
                    S.op("act", OP("activation", out=Eg[g][:, hsl, :].rearrange("p h l -> p (h l)"), in_=pb[bk][:, 0:HPM * R],
                                   func=AF.Exp), r=[PB(bk)], w=[("scr", 1)])
            for g in range(2):
                hs = slice(g * 8, g * 8 + 8)
                S.op("dve", OP("tensor_tensor", out=Wt[rows, hs, 0:R], in0=LTg[g],
                               in1=CBm[rows, g, 0:R].unsqueeze(1).to_broadcast([R, 8, R]), op=ALU.mult),
                     r=[("scr", 2), "CBm"], w=[("Wt", g)])
                S.op("dve", OP("tensor_tensor", out=CE[:, hs, 0:R], in0=Eg[g],
                               in1=xc(10 + g)[:, cs].unsqueeze(1).to_broadcast([128, 8, R]), op=ALU.mult),
                     r=[("scr", 1)] + kxc(10 + g), w=[("CE", g)])
                S.op("dve", OP("tensor_copy", out=decend[rows, hs], in_=LTg[g][:, :, R - 1]), r=[("scr", 2)], w=[("decend", g)])
                S.op("dve", OP("tensor_tensor", out=stT[l][:, g * 512:(g + 1) * 512].rearrange("p (h d) -> p h d", h=8),
                               in0=stT[l][:, g * 512:(g + 1) * 512].rearrange("p (h d) -> p h d", h=8),
                               in1=Eg[g][:, :, R - 1:R].to_broadcast([128, 8, 64]), op=ALU.mult),
                     r=[("scr", 1), ("stT", l)], w=[("stT", l)])
            S.op("dve", OP("tensor_tensor", out=xsc[rows].rearrange("p (h d) -> p h d", h=16),
                           in0=xdt[rows].rearrange("p (h d) -> p h d", h=16),
                           in1=decend[rows, :].unsqueeze(2).to_broadcast([R, 16, 64]), op=ALU.mult),
                 r=["xdt", ("decend", 0), ("decend", 1)], w=["xsc"])
            for j in range(8):
                b = j // 4
                col = slice((j % 4) * R, (j % 4 + 1) * R)
                g = j // 4
                mm(pb[b][:, col], diagD[:, l, j, :], xc(j)[:, cs], True, False, ["diagD"] + kxc(j), [PB(b)])
                for hh in range(2):
                    h = 2 * j + hh
                    prt = slice(hh * 64, hh * 64 + 64)
                    mm(pb[b][prt, col], xdt[rows, h * 64:(h + 1) * 64], Wt[rows, h, 0:R], False, False, ["xdt", ("Wt", g)], [PB(b)])
                    mm(pb[b][prt, col], stB[l][:, h * 64:(h + 1) * 64], CE[:, h, 0:R], False, True, [("stB", l), ("CE", g)], [PB(b)])
            for b in range(2):
                S.op("dve", OP("tensor_tensor", out=y2[:, b * 4:b * 4 + 4, cs],
                               in0=pb[b][:, 0:4 * R].rearrange("p (j t) -> p j t", j=4),
                               in1=zs[:, b * 4:b * 4 + 4, cs], op=ALU.mult),
                     r=[PB(b)] + [("zs", b * 4 + j) for j in range(4)], w=[("y2", b * 4 + j) for j in range(4)])
            for g in range(2):
                mm(pb[2 + g][:, :], Btok[rows, g, :], xsc[rows, g * 512:(g + 1) * 512], True, True, ["Btok", "xsc"], [PB(2 + g)])
                S.op("dve", OP("tensor_tensor", out=stT[l][:, g * 512:(g + 1) * 512], in0=stT[l][:, g * 512:(g + 1) * 512],
                               in1=pb[2 + g][:, :], op=ALU.add), r=[PB(2 + g), ("stT", l)], w=[("stT", l)])
            S.op("act", OP("activation", out=stB[l][:], in_=stT[l][:], func=AF.Copy), r=[("stT", l)], w=[("stB", l)])

        def state_out(l, dst):
            for c in range(8):
                b = 2 + c // 4
                S.op("pe", OP("transpose", pb[b][:, (c % 4) * 128:(c % 4 + 1) * 128], stT[l][:, c * 128:(c + 1) * 128], ident[:]),
                     r=[("stT", l), "ident"], w=[PB(b)])
            for b in range(2):
                S.op("dve", OP("tensor_copy", out=sso[:, b * 4:b * 4 + 4, :], in_=pb[2 + b][:, :].rearrange("p (c n) -> p c n", c=4)),
                     r=[PB(2 + b)], w=[("scr", 2)])
            S.dma("pool", ssslot, dst.rearrange("(c p) n -> p c n", p=128), sso, r=[("scr", 2)])

        def conv_out_store(l, cvi, dst):
            for c in range(12):
                S.op("pe", OP("transpose", pb[3][0:3, c * 128 % 512:c * 128 % 512 + 128], convo[:, cvi, c, :], ident[:]),
                     r=["convo", "ident"], w=[PB(3)])
                if c % 4 == 3:
                    S.op("dve", OP("tensor_copy", out=convT[:, c - 3:c + 1, :], in_=pb[3][0:3, :].rearrange("p (c q) -> p c q", c=4)),
                         r=[PB(3)], w=[("scr", 2), "mtmp0"])
            S.dma("pool", cvslot, dst.rearrange("j (c p) -> j c p", p=128), convT, r=[("scr", 2), "mtmp0"])

        def attn_finish(l, ocs, nq):
            rv = rec[:, 0:4 * nq].rearrange("p (j q) -> p j q", j=4)
            S.op("dve", OP("tensor_tensor", out=rv, in0=pb[7][:, 0:4 * nq].rearrange("p (j q) -> p j q", j=4),
                           in1=esink[:, l, :].unsqueeze(2).to_broadcast([128, 4, nq]), op=ALU.add),
                 r=[PB(7), "esink"], w=["pexp"])
            S.op("act", OP("activation", out=rec[:, 0:4 * nq], in_=rec[:, 0:4 * nq], func=AF.Ln), r=["pexp"], w=["pexp"])
            S.op("act", OP("activation", out=rec[:, 0:4 * nq], in_=rec[:, 0:4 * nq], func=AF.Exp, scale=-1.0), r=["pexp"], w=["pexp"])
            S.op("dve", OP("tensor_tensor", out=oatt[:, :, ocs], in0=pb[6][:, 0:4 * nq].rearrange("p (j q) -> p j q", j=4),
                           in1=rv, op=ALU.mult), r=[PB(6), "pexp"], w=[("oatt", j) for j in range(4)])

        def attn_prompt(l, s, gt):
            cs = slice(s * 128, (s + 1) * 128)
            blks = [1] if gt == 0 else [0, 1]

            def st(hp):
                g = hp // 2
                b = 4 + hp % 2
                for hh in range(2):
                    for blk in blks:
                        kcol = slice(s * 128 + blk * 128, s * 128 + blk * 128 + 128)
                        mm(pb[b][:, (hh * 2 + blk) * 128:(hh * 2 + blk + 1) * 128], kd[l][g][0:64, kcol], qT[:, 2 * hp + hh, cs],
                           True, True, [("kd", l, g), ("qT", 2 * hp + hh)], [PB(b)])

            def em(hp):
                b = 4 + hp % 2
                if gt == 0:
                    for hh in range(2):
                        cc = slice((hh * 2 + 1) * 128, (hh * 2 + 2) * 128)
                        S.op("act", OP("activation", out=pexp[:, cc], in_=pb[b][:, cc], func=AF.Exp, scale=0.125), r=[PB(b)], w=["pexp"])
                        S.op("dve", OP("tensor_tensor", out=pT[hp % 2][:, cc], in0=pexp[:, cc], in1=expB[:, 2 * hp + hh, 1, :], op=ALU.mult),
                             r=["pexp", "expB"], w=[("pT", hp % 2)])
                else:
                    S.op("act", OP("activation", out=pexp[:], in_=pb[b][:, :], func=AF.Exp, scale=0.125), r=[PB(b)], w=["pexp"])
                    S.op("dve", OP("tensor_tensor", out=pT[hp % 2][:], in0=pexp[:],
                                   in1=expB[:, 2 * hp:2 * hp + 2].rearrange("p h b q -> p (h b q)"), op=ALU.mult),
                         r=["pexp", "expB"], w=[("pT", hp % 2)])

            def pv(hp):
                g = hp // 2
                for hh in range(2):
                    prt = slice(hh * 64, hh * 64 + 64)
                    for bi, blk in enumerate(blks):
                        pcol = slice((hh * 2 + blk) * 128, (hh * 2 + blk + 1) * 128)
                        mm(pb[6][prt, hp * 128:(hp + 1) * 128], vt[l][:, s + blk, g * 64:(g + 1) * 64], pT[hp % 2][:, pcol],
                           bi == 0, bi == len(blks) - 1, [("vt", l, s + blk), ("pT", hp % 2)], [PB(6)])
                    for bi, blk in enumerate(blks):
                        pcol = slice((hh * 2 + blk) * 128, (hh * 2 + blk + 1) * 128)
                        mm(pb[7][prt, hp * 128:(hp + 1) * 128], onesb[:, 0:64], pT[hp % 2][:, pcol],
                           bi == 0, bi == len(blks) - 1, ["onesb", ("pT", hp % 2)], [PB(7)])

            st(0)
            st(1)
            for hp in range(4):
                em(hp)
                pv(hp)
                if hp + 2 < 4:
                    st(hp + 2)
            attn_finish(l, cs, 128)

        def attn_sample(l, bq):
            qs = slice(bq * 16, bq * 16 + 16)
            S.dma("sp", xslot[0], stg[:, 0:128], cache_k[l, bq], w=SK)
            S.dma("sp", xslot[1], stg[:, 128:256], cache_v[l, bq], w=SK)
            for g in range(2):
                S.op("pe", OP("transpose", pb[4][0:64, g * 128:(g + 1) * 128], stg[:, g * 64:(g + 1) * 64], ident[:]),
                     r=SK + ["ident"], w=[PB(4)])
                S.op("dve", OP("tensor_copy", out=kd[l][g][0:64, 0:128], in_=pb[4][0:64, g * 128:(g + 1) * 128]), r=[PB(4)],
                     w=[("kd", l, g)])
            S.op("dve", OP("tensor_copy", out=vt[l][:, 0, :], in_=stg[:, 128:256]), r=SK, w=[("vt", l, 0)])
            S.dma("pool", kcslot, k_samp[l, bq, 0:112, :], cache_k[l, bq, 16:128, :])
            S.dma("pool", kcslot, v_samp[l, bq, 0:112, :], cache_v[l, bq, 16:128, :])
            for h in range(8):
                g = h // 4
                mm(pb[5][:, h * 32:h * 32 + 16], kd[l][g][0:64, 0:128], qT[:, h, qs], True, True, [("kd", l, g), ("qT", h)], [PB(5)])
                mm(pb[5][0:16, h * 32 + 16:h * 32 + 32], kd[l][g][0:64, 128 + bq * 16:128 + bq * 16 + 16], qT[:, h, qs], True, True,
                   [("kd", l, g), ("qT", h)], [PB(5)])
            pe3 = pexp[:, 0:256].rearrange("p (h c) -> p h c", h=8)
            ps3 = pb[5][:, 0:256].rearrange("p (h c) -> p h c", h=8)
            pt3 = pT[0][:, 0:256].rearrange("p (h c) -> p h c", h=8)
            S.op("act", OP("activation", out=pe3[:, :, 0:16], in_=ps3[:, :, 0:16], func=AF.Exp, scale=0.125), r=[PB(5)], w=["pexp"])
            S.op("act", OP("activation", out=pe3[0:16, :, 16:32], in_=ps3[0:16, :, 16:32], func=AF.Exp, scale=0.125), r=[PB(5)], w=["pexp"])
            S.op("dve", OP("tensor_tensor", out=pt3[:, :, 0:16], in0=pe3[:, :, 0:16], in1=expB[:, :, 0, 0:16], op=ALU.mult),
                 r=["pexp", "expB"], w=[("pT", 0)])
            S.op("dve", OP("tensor_tensor", out=pt3[0:16, :, 16:32], in0=pe3[0:16, :, 16:32], in1=expB[0:16, :, 1, 0:16], op=ALU.mult),
                 r=["pexp", "expB"], w=[("pT", 0)])
            for h in range(8):
                g, hp, hh = h // 4, h // 2, h % 2
                prt = slice(hh * 64, hh * 64 + 64)
                oc_ = slice(hp * 16, hp * 16 + 16)
                mm(pb[6][prt, oc_], vt[l][:, 0, g * 64:(g + 1) * 64], pt3[:, h, 0:16], True, False, [("vt", l, 0), ("pT", 0)], [PB(6)])
                mm(pb[6][prt, oc_], vt[l][0:16, 1 + bq, g * 64:(g + 1) * 64], pt3[0:16, h, 16:32], False, True,
                   [("vt", l, 1 + bq), ("pT", 0)], [PB(6)])
                mm(pb[7][prt, oc_], onesb[:, 0:64], pt3[:, h, 0:16], True, False, ["onesb", ("pT", 0)], [PB(7)])
                mm(pb[7][prt, oc_], onesb[0:16, 0:64], pt3[0:16, h, 16:32], False, True, ["onesb", ("pT", 0)], [PB(7)])
            attn_finish(l, qs, 16)

        def layer(l, mode, ti):
            prompt = mode == "prompt"
            n = T if prompt else 32
            last = prompt and ti == NT - 1
            hn_rhs = lambda kc: hn[:, kc, 0:n]
            hn_keys = lambda kc: [("hn", kc)]
            rmsnorm_to(hn, lambda kc: ("hn", kc), lambda kc: ptA[:, l * 8 + kc:l * 8 + kc + 1], n)
            wv, wk = wget(l, "q")
            for h in range(8):
                b = dense_bank()
                for kc in range(KC):
                    mm(pb[b][0:64, 0:n], wv[:, kc, h * 64:(h + 1) * 64], hn[:, kc, 0:n], kc == 0, kc == KC - 1, [wk, ("hn", kc)], [PB(b)])
                if h % 2 == 0:
                    S.op("act", OP("activation", out=qT[:, h, 0:n], in_=pb[b][0:64, 0:n], func=AF.Copy), r=[PB(b)], w=[("qT", h)])
                else:
                    S.op("dve", OP("tensor_copy", out=qT[:, h, 0:n], in_=pb[b][0:64, 0:n]), r=[PB(b)], w=[("qT", h)])
            wv, wk = wget(l, "kv")
            for g in range(2):
                b = dense_bank()
                for kc in range(KC):
                    mm(pb[b][0:64, 0:n], wv[:, kc, g * 128:g * 128 + 64], hn[:, kc, 0:n], kc == 0, kc == KC - 1, [wk, ("hn", kc)], [PB(b)])
                S.op("dve", OP("tensor_copy", out=kd[l][g][0:64, 128:128 + n], in_=pb[b][0:64, 0:n]), r=[PB(b)], w=[("kd", l, g)])
            if prompt:
                for s in range(NSUB):
                    for kc in range(KC):
                        mm(pb[6][:, 0:256], hn[:, kc, s * 128:(s + 1) * 128], wv[:, kc, 256:512], kc == 0, kc == KC - 1,
                           [wk, ("hn", kc)], [PB(6)])
                    S.op("dve", OP("tensor_copy", out=vt[l][:, s + 1, :], in_=pb[6][:, 128:256]), r=[PB(6)], w=[("vt", l, s + 1)])
                    if last and s == NSUB - 1:
                        S.op("dve", OP("tensor_copy", out=kvo[:], in_=pb[6][:, 0:256]), r=[PB(6)], w=["kvo"])
                        S.dma("pool", kvslot, k_out[l], kvo[:, 0:128], r=["kvo"])
                        S.dma("pool", kvslot, v_out[l], kvo[:, 128:256], r=["kvo"])
            else:
                for bq in range(2):
                    for kc in range(KC):
                        mm(pb[6][0:16, 0:256], hn[:, kc, bq * 16:(bq + 1) * 16], wv[:, kc, 256:512], kc == 0, kc == KC - 1,
                           [wk, ("hn", kc)], [PB(6)])
                    S.op("dve", OP("tensor_copy", out=vt[l][0:16, 1 + bq, :], in_=pb[6][0:16, 128:256]), r=[PB(6)], w=[("vt", l, 1 + bq)])
                    S.op("dve", OP("tensor_copy", out=kvo[0:16, :], in_=pb[6][0:16, 0:256]), r=[PB(6)], w=["kvo"])
                    S.dma("pool", kvslot, k_samp[l, bq, 112:128, :], kvo[0:16, 0:128], r=["kvo"])
                    S.dma("pool", kvslot, v_samp[l, bq, 112:128, :], kvo[0:16, 128:256], r=["kvo"])
            for zi in range(2):
                wv, wk = wget(l, f"z{zi}")
                for oc in range(4):
                    b = dense_oc(wv, wk, oc, KC, hn_rhs, hn_keys, n)
                    evac_act(zs[:, zi * 4 + oc, 0:n], b, AF.Silu, [("zs", zi * 4 + oc)], n)
            if prompt:
                for c in range(12):
                    S.op("dve", OP("tensor_copy", out=xbcp(c)[:, 0:3], in_=hist[l][:, c, :]), r=[("hist", l)], w=kxbcp(c))
            else:
                S.dma("sp", xslot[0], stg[0:72, 0:128], state_conv[l].rearrange("r (c p) -> (r c) p", p=128), w=SK)
                S.op("pe", OP("transpose", pb[4][:, 0:72], stg[0:72, 0:128], ident[0:72, 0:72]), r=SK + ["ident"], w=[PB(4)])
                hv = pb[4][:, 0:72].rearrange("p (b j c) -> p b j c", b=2, j=3)
                for c in range(12):
                    S.op("dve", OP("tensor_copy", out=xbcp(c)[:, 0:38].rearrange("p (b t) -> p b t", b=2)[:, :, 0:3], in_=hv[:, :, :, c]),
                         r=[PB(4)], w=kxbcp(c))
            for xi, nm in enumerate(("xs0", "xs1", "bc")):
                wv, wk = wget(l, nm)
                for oc in range(4):
                    c = xi * 4 + oc
                    b = dense_oc(wv, wk, oc, KC, hn_rhs, hn_keys, n)
                    if prompt:
                        evac_copy(xbcp(c)[:, 3:515], b, kxbcp(c))
                        if last:
                            S.op("dve", OP("tensor_copy", out=convo[:, 0, c, :], in_=pb[b][:, T - 3:T]), r=[PB(b)], w=["convo"])
                    else:
                        S.op("dve", OP("tensor_copy", out=xbcp(c)[:, 0:38].rearrange("p (b t) -> p b t", b=2)[:, :, 3:19],
                                       in_=pb[b][:, 0:32].rearrange("p (b t) -> p b t", b=2)), r=[PB(b)], w=kxbcp(c))
                        S.op("act", OP("activation", out=convo[:, :, c, :], in_=pb[b][:, 0:32].rearrange("p (b t) -> p b t", b=2)[:, :, 13:16],
                                       func=AF.Copy), r=[PB(b)], w=["convo"])
            if last:
                conv_out_store(l, 0, conv_out[l])
            if prompt:
                for c in range(12):
                    S.op("dve", OP("tensor_copy", out=hist[l][:, c, :], in_=xbcp(c)[:, T:T + 3]), r=kxbcp(c), w=[("hist", l)])
            else:
                for bq in range(2):
                    conv_out_store(l, bq, conv_samp[l, bq])
            wdt, wdk = wget(l, "dt")
            if prompt:
                for s in range(NSUB):
                    for kc in range(KC):
                        mm(pb[7][:, s * 16:(s + 1) * 16], hn[:, kc, s * 128:(s + 1) * 128], wdt[:, kc, 0:16], kc == 0, kc == KC - 1,
                           [wdk, ("hn", kc)], [PB(7)])
                S.op("dve", OP("tensor_tensor", out=dtraw[:], in0=pb[7][:, 0:64].rearrange("p (s h) -> p s h", s=NSUB),
                               in1=dtb[:, l, :].unsqueeze(1).to_broadcast([128, NSUB, 16]), op=ALU.add), r=[PB(7), "dtb"], w=["dtraw"])
            else:
                for bq in range(2):
                    for kc in range(KC):
                        mm(pb[7][0:16, bq * 16:(bq + 1) * 16], hn[:, kc, bq * 16:(bq + 1) * 16], wdt[:, kc, 0:16], kc == 0, kc == KC - 1,
                           [wdk, ("hn", kc)], [PB(7)])
                S.op("dve", OP("tensor_tensor", out=dtraw[0:16, 0:2, :], in0=pb[7][0:16, 0:32].rearrange("p (s h) -> p s h", s=2),
                               in1=dtb[0:16, l, :].unsqueeze(1).to_broadcast([16, 2, 16]), op=ALU.add), r=[PB(7), "dtb"], w=["dtraw"])
            cn = n if prompt else 35
            def conv_chunk(c):
                tmp = ctmp[c % 2]
                S.op("act", OP("activation", out=tmp[:, 0:cn], in_=xbcp(c)[:, 0:cn], func=AF.Copy, scale=ptB[:, l * 48 + c:l * 48 + c + 1]),
                     r=kxbcp(c) + ["ptB"], w=[("ctmp", c % 2)])
                for j in range(1, 4):
                    S.op("dve", OP("scalar_tensor_tensor", out=tmp[:, 0:cn], in0=xbcp(c)[:, j:j + cn],
                                   scalar=ptB[:, l * 48 + j * 12 + c:l * 48 + j * 12 + c + 1], in1=tmp[:, 0:cn],
                                   op0=ALU.mult, op1=ALU.add), r=kxbcp(c) + ["ptB", ("ctmp", c % 2)], w=[("ctmp", c % 2)])
                cbk = ptA[:, 56 + l * 12 + c:56 + l * 12 + c + 1]
                segs = [(slice(0, T), slice(0, T))] if prompt else [(slice(0, 16), slice(0, 16)), (slice(16, 32), slice(19, 35))]
                for od, sd in segs:
                    S.op("act", OP("activation", out=xc(c)[:, od], in_=tmp[:, sd], func=AF.Sigmoid, bias=cbk),
                         r=[("ctmp", c % 2), "ptA"], w=kxc(c))
                for od, sd in segs:
                    S.op("dve", OP("scalar_tensor_tensor", out=xc(c)[:, od], in0=tmp[:, sd], scalar=cbk, in1=xc(c)[:, od],
                                   op0=ALU.add, op1=ALU.mult), r=[("ctmp", c % 2), "ptA"] + kxc(c), w=kxc(c))
            cq = list(range(12))
            for gt_, gname in ((ga, "ga"), (gs, "gs")):
                for hi in range(2):
                    wv, wk = wget(l, f"{gname}{hi}")
                    for oc in range(4):
                        b = dense_oc(wv, wk, oc, KC, hn_rhs, hn_keys, n)
                        evac_act(gt_[:, hi * 4 + oc, 0:n], b, AF.Sigmoid, [(gname, hi * 4 + oc)], n)
                        if cq:
                            conv_chunk(cq.pop(0))
            while cq:
                conv_chunk(cq.pop(0))
            if prompt:
                for s in range(NSUB):
                    attn_prompt(l, s, ti * NSUB + s)
                    ssd_block(l, 128, slice(s * 128, (s + 1) * 128), s)
                if not last:
                    for g in range(2):
                        S.op("dve", OP("tensor_copy", out=kd[l][g][0:64, 0:128], in_=kd[l][g][0:64, T:T + 128]), r=[("kd", l, g)],
                             w=[("kd", l, g)])
                    S.op("dve", OP("tensor_copy", out=vt[l][:, 0, :], in_=vt[l][:, NSUB, :]), r=[("vt", l, NSUB)], w=[("vt", l, 0)])
                else:
                    state_out(l, ssm_out[l])
            else:
                for bq in range(2):
                    attn_sample(l, bq)
                    S.dma("sp", xslot[0], stg[:, :].rearrange("p (c n) -> p c n", c=8),
                          state_ssm[l, bq].rearrange("(c p) n -> p c n", p=128), w=SK)
                    for c in range(8):
                        b = 2 + c // 4
                        S.op("pe", OP("transpose", pb[b][:, (c % 4) * 128:(c % 4 + 1) * 128], stg[:, c * 128:(c + 1) * 128], ident[:]),
                             r=SK + ["ident"], w=[PB(b)])
                    for b in range(2):
                        S.op("dve", OP("tensor_copy", out=stT[l][:, b * 512:(b + 1) * 512], in_=pb[2 + b][:, :]), r=[PB(2 + b)],
                             w=[("stT", l)])
                    S.op("act", OP("activation", out=stB[l][:], in_=stT[l][:], func=AF.Copy), r=[("stT", l)], w=[("stB", l)])
                    ssd_block(l, 16, slice(bq * 16, bq * 16 + 16), bq)
                    state_out(l, ssm_samp[l, bq])
            for kc in range(KC):
                S.op("act", OP("activation", out=sq[:, kc, 0:n], in_=y2[:, kc, 0:n], func=AF.Square), r=[("y2", kc)],
                     w=[("scr", 0), ("scr", 1)])
            for kc in range(KC):
                mm(pb[2][:, 0:n], onesb[:], sq[:, kc, 0:n], kc == 0, kc == KC - 1, ["onesb", ("scr", 0), ("scr", 1)], [PB(2)])
            rsq(pb[2][:, 0:n], rstd[:, 0:n], PB(2))
            for kc in range(KC):
                S.op("dve", OP("scalar_tensor_tensor", out=hn[:, kc, 0:n], in0=y2[:, kc, 0:n],
                               scalar=ptA[:, 32 + l * 8 + kc:32 + l * 8 + kc + 1], in1=rstd[:, 0:n], op0=ALU.mult, op1=ALU.mult),
                     r=[("y2", kc), "rstd", "ptA"], w=[("hn", kc)])
            for hi in range(2):
                wa, wak = wget(l, f"ao{hi}")
                wsv, wsk = wget(l, f"so{hi}", hold=1)
                for oc in range(4):
                    o8 = hi * 4 + oc
                    ba = dense_oc(wa, wak, oc, 4, lambda kc: oatt[:, kc, 0:n], lambda kc: [("oatt", kc)], n)
                    S.op("dve", OP("tensor_tensor", out=mtmp[0][:, 0:n], in0=pb[ba][:, 0:n], in1=ga[:, o8, 0:n], op=ALU.mult),
                         r=[PB(ba), ("ga", o8)], w=["mtmp0"])
                    bs = dense_oc(wsv, wsk, oc, KC, hn_rhs, hn_keys, n)
                    S.op("dve", OP("tensor_tensor", out=mtmp[1][:, 0:n], in0=pb[bs][:, 0:n], in1=gs[:, o8, 0:n], op=ALU.mult),
                         r=[PB(bs), ("gs", o8)], w=["mtmp1"])
                    S.op("dve", OP("tensor_tensor", out=zs[:, o8, 0:n], in0=mtmp[0][:, 0:n], in1=mtmp[1][:, 0:n], op=ALU.add),
                         r=["mtmp0", "mtmp1"], w=[("zs", o8)])
            for hi in range(2):
                wv, wk = wget(l, f"wo{hi}")
                for oc in range(4):
                    o8 = hi * 4 + oc
                    b = dense_oc(wv, wk, oc, KC, lambda kc: zs[:, kc, 0:n], lambda kc: [("zs", kc)], n)
                    S.op("dve", OP("tensor_tensor", out=x[:, o8, 0:n], in0=x[:, o8, 0:n], in1=pb[b][:, 0:n], op=ALU.add),
                         r=[PB(b), ("x", o8)], w=[("x", o8)])
            rmsnorm_to(hn, lambda kc: ("hn", kc), lambda kc: ptA[:, 16 + l * 8 + kc:16 + l * 8 + kc + 1], n)
            for j in range(6):
                nocs = 4 if j < 5 else 2
                wg, wgk = wget(l, f"g{j}")
                wu, wuk = wget(l, f"u{j}", hold=1)
                for oc in range(nocs):
                    c = j * 4 + oc
                    bg = dense_oc(wg, wgk, oc, KC, hn_rhs, hn_keys, n)
                    S.op("act", OP("activation", out=ctmp[c % 2][:, 0:n], in_=pb[bg][:, 0:n], func=AF.Silu), r=[PB(bg)], w=[("ctmp", c % 2)])
                    bu = dense_oc(wu, wuk, oc, KC, hn_rhs, hn_keys, n)
                    S.op("dve", OP("tensor_tensor", out=actv(c)[:, 0:n], in0=pb[bu][:, 0:n], in1=ctmp[c % 2][:, 0:n], op=ALU.mult),
                         r=[PB(bu), ("ctmp", c % 2)], w=kact(c))
            for oc in range(8):
                wv, wk = wget(l, f"d{oc}")
                b = dense_bank()
                for kc in range(22):
                    mm(pb[b][:, 0:n], wv[:, kc, :], actv(kc)[:, 0:n], kc == 0, kc == 21, [wk] + kact(kc), [PB(b)])
                S.op("dve", OP("tensor_tensor", out=x[:, oc, 0:n], in0=x[:, oc, 0:n], in1=pb[b][:, 0:n], op=ALU.add),
                     r=[PB(b), ("x", oc)], w=[("x", oc)])

        for ti in (range(NT) if STAGE >= 2 else []):
            for s in range(NSUB):
                r0 = ti * T + s * 128
                load_tokens(x_prompt[r0:r0 + 128, :], 128, s * 128, xslot[s % 2])
            for l in range(DEPTH):
                layer(l, "prompt", ti)
            final_norm(T)
            for s in range(NSUB):
                r0 = ti * T + s * 128
                store_tokens(y_prompt[r0:r0 + 128, :], 128, s * 128, yslot[s % 2])
        if STAGE >= 3:
            load_tokens(x_sample[:, :], 32, 0, xslot[0])
            for l in range(DEPTH):
                layer(l, "sample", None)
            final_norm(32)
            store_tokens(y_sample[:, :], 32, 0, yslot[0])

        S.final_wait("pool", allslots)
        S.emit_all()
    return nc


_CACHE = {}


def kernel(**inputs):
    consts = static_consts()
    bucket = consts.pop("c_bucket")
    seq = inputs["x_prompt"].shape[1]
    key = (seq,)
    if key not in _CACHE:
        _CACHE[key] = build(seq, 2, bucket)
    nc = _CACHE[key]
    f = lambda a: np.ascontiguousarray(np.asarray(a, dtype=np.float32))
    shared = {k: f(inputs[k]) for k in ("rel_table", "g_mix", "w_in", "conv_w", "conv_b", "dt_bias", "a_log", "d_skip",
                                         "g_ssd", "sinks", "w_att_out", "w_ssd_out", "w_out", "g_ffn", "w_gate", "w_up",
                                         "w_down", "g_final")}
    shared.update(consts)
    xp = f(inputs["x_prompt"])
    nb = xp.shape[0]
    xs = f(inputs["x_sample"])
    ck, cv = f(inputs["cache_k"]), f(inputs["cache_v"])
    sc, sm = f(inputs["state_conv"]), f(inputs["state_ssm"])
    in_maps = []
    for c in range(NCORES):
        m = dict(shared)
        m["x_prompt"] = xp[c % nb]
        sl = slice(2 * c, 2 * c + 2)
        m["x_sample"] = np.ascontiguousarray(xs[sl].reshape(32, D))
        m["cache_k"] = np.ascontiguousarray(ck[:, sl].reshape(DEPTH, 2, 128, 128))
        m["cache_v"] = np.ascontiguousarray(cv[:, sl].reshape(DEPTH, 2, 128, 128))
        m["state_conv"] = np.ascontiguousarray(sc[:, sl].reshape(DEPTH, 6, 1536))
        m["state_ssm"] = np.ascontiguousarray(sm[:, sl].reshape(DEPTH, 2, 1024, 128))
        in_maps.append(m)
    if os.environ.get("KTRACE"):
        res = run_bass_kernel_spmd(nc, in_maps, core_ids=list(range(NCORES)), trace=True)
        print("EXEC_NS", res.exec_time_ns)
    else:
        res = run_bass_kernel_spmd(nc, in_maps, core_ids=list(range(NCORES)))
    R = res.results
    y_prompt = np.stack([R[b]["y_prompt"] for b in range(nb)])
    k_prompt = np.stack([R[b]["k_prompt"] for b in range(nb)], axis=1).reshape(DEPTH, nb, 128, 2, 64)
    v_prompt = np.stack([R[b]["v_prompt"] for b in range(nb)], axis=1).reshape(DEPTH, nb, 128, 2, 64)
    conv_prompt = np.stack([R[b]["conv_prompt"] for b in range(nb)], axis=1)
    ssm_prompt = np.stack([R[b]["ssm_prompt"] for b in range(nb)], axis=1).reshape(DEPTH, nb, 16, 64, 128)
    y_sample = np.concatenate([R[c]["y_sample"].reshape(2, 16, D) for c in range(NCORES)], axis=0)
    k_sample = np.concatenate([R[c]["k_sample"].reshape(DEPTH, 2, 128, 2, 64) for c in range(NCORES)], axis=1)
    v_sample = np.concatenate([R[c]["v_sample"].reshape(DEPTH, 2, 128, 2, 64) for c in range(NCORES)], axis=1)
    conv_sample = np.concatenate([R[c]["conv_sample"] for c in range(NCORES)], axis=1)
    ssm_sample = np.concatenate([R[c]["ssm_sample"].reshape(DEPTH, 2, 16, 64, 128) for c in range(NCORES)], axis=1)
    return (y_prompt, y_sample, k_prompt, v_prompt, conv_prompt, ssm_prompt, k_sample, v_sample, conv_sample, ssm_sample)
```
